# Optimizing a Trainium2 kernel written in Bass

```python
import jax, jax.numpy as jnp
from jax import lax
import numpy as np

D_MODEL = 2048
BATCH = 8
SEQ = 4096
DEPTH = 4

MIX_W = D_MODEL
ATT_W = MIX_W // 2
HEAD_DIM = 128
N_ATT_HEADS = ATT_W // HEAD_DIM
CONV_W = MIX_W // 4
CONV_TAPS = 3
POOL_W = MIX_W - ATT_W - CONV_W
POOL_WINDOWS = (2, 4, 8, 16)
POOL_GROUP = POOL_W // len(POOL_WINDOWS)
Q_BLOCK = 128
LN_EPS = 1e-5
DEEPNORM_ALPHA = (2 * DEPTH) ** 0.25
DEEPNORM_BETA = (8 * DEPTH) ** -0.25

IN_WIDTHS = (ATT_W, ATT_W, ATT_W, ATT_W, N_ATT_HEADS,
             CONV_W, CONV_W, CONV_W, CONV_W, POOL_W, POOL_W)
IN_W = sum(IN_WIDTHS)
IN_SPLITS = tuple(sum(IN_WIDTHS[:i + 1]) for i in range(len(IN_WIDTHS) - 1))

kernel_name = "hybrid_fox_shortconv_pool_deepnorm"


def layer_norm(x, g, b):
    x32 = x.astype(jnp.float32)
    mu = jnp.mean(x32, axis=-1, keepdims=True)
    var = jnp.mean(jnp.square(x32 - mu), axis=-1, keepdims=True)
    return ((x32 - mu) * lax.rsqrt(var + LN_EPS) * g + b).astype(x.dtype)


def forgetting_attention(q, k, v, fg_logit, b_f):
    b, s, h, dh = q.shape
    nb = s // Q_BLOCK
    log_f = jax.nn.log_sigmoid(fg_logit.astype(jnp.float32) + b_f.astype(jnp.float32))
    cum = jnp.cumsum(log_f, axis=1)
    cum_k = jnp.transpose(cum, (0, 2, 1))
    q_blocks = q.reshape(b, nb, Q_BLOCK, h, dh).transpose(1, 0, 2, 3, 4)
    c_blocks = cum.reshape(b, nb, Q_BLOCK, h).transpose(1, 0, 3, 2)
    k_pos = jnp.arange(s)
    scale = HEAD_DIM ** -0.5

    def block(args):
        qb, cb, i = args
        logits = jnp.einsum('bqhd,bkhd->bhqk', qb, k).astype(jnp.float32) * scale
        logits = logits + cb[..., None] - cum_k[:, :, None, :]
        q_pos = i * Q_BLOCK + jnp.arange(Q_BLOCK)
        causal = k_pos[None, :] <= q_pos[:, None]
        logits = jnp.where(causal, logits, -jnp.inf)
        p = jax.nn.softmax(logits, axis=-1).astype(v.dtype)
        return jnp.einsum('bhqk,bkhd->bqhd', p, v)

    out = lax.map(block, (q_blocks, c_blocks, jnp.arange(nb)))
    return out.transpose(1, 0, 2, 3, 4).reshape(b, s, h * dh)


def short_conv_mixer(gate_b, gate_c, h, conv_w):
    s = h.shape[1]
    u = gate_c * h
    up = jnp.pad(u, ((0, 0), (CONV_TAPS - 1, 0), (0, 0)))
    y = conv_w[0] * up[:, 0:s]
    for j in range(1, CONV_TAPS):
        y = y + conv_w[j] * up[:, j:j + s]
    return gate_b * y


def multiscale_pool_mixer(u, pool_w, pool_scale):
    s = u.shape[1]
    u32 = u.astype(jnp.float32)
    cs = jnp.pad(jnp.cumsum(u32, axis=1), ((0, 0), (1, 0), (0, 0)))
    t1 = jnp.arange(1, s + 1, dtype=jnp.float32)
    outs = []
    for g, w in enumerate(POOL_WINDOWS):
        sl = slice(g * POOL_GROUP, (g + 1) * POOL_GROUP)
        csg = cs[:, :, sl]
        lagged = jnp.pad(csg[:, :s - w + 1], ((0, 0), (w - 1, 0), (0, 0)))
        count = jnp.minimum(t1, float(w))
        mean = (csg[:, 1:] - lagged) / count[None, :, None]
        z = (mean - u32[:, :, sl]).astype(u.dtype)
        outs.append(jnp.einsum('bsc,cd->bsd', z, pool_w[g]))
    return jnp.concatenate(outs, axis=-1) * pool_scale


def setup_inputs(seed: int = 0) -> dict:
    key = jax.random.key(seed)
    ks = jax.random.split(key, 10)
    x = jax.random.normal(ks[0], (BATCH, SEQ, D_MODEL), jnp.float32)
    w_in = jax.random.normal(ks[1], (DEPTH, D_MODEL, IN_W), jnp.float32) * D_MODEL ** -0.5
    b_f = jax.random.uniform(ks[2], (DEPTH, N_ATT_HEADS), jnp.float32, 1.0, 4.0)
    conv_w = jax.random.normal(ks[3], (DEPTH, CONV_TAPS, CONV_W), jnp.float32) * CONV_TAPS ** -0.5
    pool_w = jax.random.normal(ks[4], (DEPTH, len(POOL_WINDOWS), POOL_GROUP, POOL_GROUP),
                               jnp.float32) * POOL_GROUP ** -0.5
    pool_scale = 1.0 + 0.1 * jax.random.normal(ks[5], (DEPTH, POOL_W), jnp.float32)
    w_out = jax.random.normal(ks[6], (DEPTH, MIX_W, D_MODEL), jnp.float32) * (
        MIX_W ** -0.5 * DEEPNORM_BETA)
    ln_g = 1.0 + 0.1 * jax.random.normal(ks[7], (DEPTH, D_MODEL), jnp.float32)
    ln_b = 0.02 * jax.random.normal(ks[8], (DEPTH, D_MODEL), jnp.float32)
    return {"x": x, "w_in": w_in, "b_f": b_f, "conv_w": conv_w, "pool_w": pool_w,
            "pool_scale": pool_scale, "w_out": w_out, "ln_g": ln_g, "ln_b": ln_b}


def reference(x, w_in, b_f, conv_w, pool_w, pool_scale, w_out, ln_g, ln_b):
    b, s, _ = x.shape
    for l in range(DEPTH):
        proj = jnp.einsum('bsd,de->bse', x, w_in[l])
        (q, k, v, g_att, fg, c_b, c_c, c_h, g_conv, p_u, g_pool) = jnp.split(
            proj, IN_SPLITS, axis=-1)
        heads = (b, s, N_ATT_HEADS, HEAD_DIM)
        y_att = forgetting_attention(q.reshape(heads), k.reshape(heads), v.reshape(heads),
                                     fg, b_f[l]) * jax.nn.silu(g_att)
        y_conv = short_conv_mixer(c_b, c_c, c_h, conv_w[l]) * jax.nn.silu(g_conv)
        y_pool = multiscale_pool_mixer(p_u, pool_w[l], pool_scale[l]) * jax.nn.silu(g_pool)
        y = jnp.concatenate([y_att, y_conv, y_pool], axis=-1)
        y = jnp.einsum('bse,ed->bsd', y, w_out[l])
        x = layer_norm(DEEPNORM_ALPHA * x + y, ln_g[l], ln_b[l])
    return x
```

```python
import numpy as np
from contextlib import ExitStack
import concourse.bass as bass
import concourse.mybir as mybir
from concourse.bass_utils import run_bass_kernel_spmd

F32 = mybir.dt.float32
BF16 = mybir.dt.bfloat16
ALU = mybir.AluOpType
AF = mybir.ActivationFunctionType

FUSED = True
DEPTH = 4
D = 2048
SEQ = 4096
KC = 16
NT = 8
IN_W = 7176
Q0, K0, V0, GA0, FG0, CB0, CC0, CH0, GC0, PU0, GP0 = 0, 1024, 2048, 3072, 4096, 4104, 4616, 5128, 5640, 6152, 6664
SCALE = 128 ** -0.5
ALPHA = (2 * DEPTH) ** 0.25
LN_EPS = 1e-5
WINDOWS = (2, 4, 8, 16)


class Op:
    __slots__ = ("eng", "fn", "deps", "dma", "sem", "val", "sig", "epoch")


class Sched:
    def __init__(self, nc, stack, n_dma_sems=None):
        self.nc = nc
        self.stack = stack
        self.ops = []
        self.lastw = {}
        self.rd_c = {}
        self.rd_d = {}
        self.epoch = 0
        self.prog = {}
        nd = n_dma_sems or {"sp": 8, "pool": 6}
        self.dpool = {q: [[stack.enter_context(nc.semaphore(f"d_{q}{i}")), 0, None] for i in range(n)]
                      for q, n in nd.items()}
        self.dnext = {q: 0 for q in nd}

    def _prog(self, eng, epoch):
        k = (eng, epoch)
        if k not in self.prog:
            self.prog[k] = [self.stack.enter_context(self.nc.semaphore(f"p_{eng}{epoch}")), 0]
        return self.prog[k]

    def op(self, eng, fn, r=(), w=(), dma=False, extra=()):
        o = Op()
        o.eng, o.fn, o.dma, o.sig, o.epoch = eng, fn, dma, False, self.epoch
        o.sem = None
        o.val = 0
        deps = {}

        def add(d, kind):
            if d is None or d is o:
                return
            if (not dma) and (not d.dma) and d.eng == eng:
                if eng == "pe" or kind != "raw":
                    return
            deps[id(d)] = d

        for k in r:
            add(self.lastw.get(k), "raw")
        for k in w:
            add(self.lastw.get(k), "waw")
            for d in self.rd_c.get(k, {}).values():
                add(d, "war")
            for d in self.rd_d.get(k, ()):
                add(d, "war")
        for d in extra:
            add(d, "raw")
        if dma:
            slot = self.dpool[eng][self.dnext[eng] % len(self.dpool[eng])]
            self.dnext[eng] += 1
            if slot[2] is not None:
                deps[id(slot[2])] = slot[2]
            slot[1] += 1
            slot[2] = o
            o.sem = slot[0]
            o.val = 16 * slot[1]
        o.deps = list(deps.values())
        for k in w:
            self.lastw[k] = o
            self.rd_c[k] = {}
            self.rd_d[k] = []
        for k in r:
            if dma:
                self.rd_d.setdefault(k, []).append(o)
            else:
                self.rd_c.setdefault(k, {})[eng] = o
        self.ops.append(o)
        return o

    def finalize(self):
        for o in self.ops:
            for d in o.deps:
                d.sig = True
        for o in self.ops:
            if not o.dma and o.sig:
                p = self._prog(o.eng, o.epoch)
                p[1] += 1
                o.sem = p[0]
                o.val = p[1]

    def emit(self, engname, eng):
        seen = {}
        for o in self.ops:
            if o.eng != engname:
                continue
            for d in o.deps:
                if seen.get(id(d.sem), 0) < d.val:
                    eng.wait_ge(d.sem, d.val)
                    seen[id(d.sem)] = d.val
            if o.fn is None:
                continue
            ins = o.fn(eng)
            if o.dma:
                ins.then_inc(o.sem, 16)
            elif o.sig:
                ins.then_inc(o.sem, 1)

    def run(self):
        self.finalize()
        with self.nc.Block() as block:
            @block.tensor
            def _(e):
                self.emit("pe", e)

            @block.scalar
            def _(e):
                self.emit("act", e)

            @block.vector
            def _(e):
                self.emit("dve", e)

            @block.gpsimd
            def _(e):
                self.emit("pool", e)

            @block.sync
            def _(e):
                self.emit("sp", e)


def build_nc(L, debug=False):
    nc = bass.Bass("TRN2", target_bir_lowering=False)
    x_in = nc.dram_tensor("x", [SEQ, D], F32, kind="ExternalInput").ap()
    w_in = nc.dram_tensor("w_in", [L, D, IN_W], F32, kind="ExternalInput").ap()
    b_f = nc.dram_tensor("b_f", [L, 8], F32, kind="ExternalInput").ap()
    conv_w = nc.dram_tensor("conv_w", [L, 3, 512], F32, kind="ExternalInput").ap()
    pool_w = nc.dram_tensor("pool_w", [L, 4, 128, 128], F32, kind="ExternalInput").ap()
    pool_scale = nc.dram_tensor("pool_scale", [L, 512], F32, kind="ExternalInput").ap()
    w_out = nc.dram_tensor("w_out", [L, D, D], F32, kind="ExternalInput").ap()
    ln_g = nc.dram_tensor("ln_g", [L, D], F32, kind="ExternalInput").ap()
    ln_b = nc.dram_tensor("ln_b", [L, D], F32, kind="ExternalInput").ap()
    cst = nc.dram_tensor("cst", [128, 272], F32, kind="ExternalInput").ap()
    out = nc.dram_tensor("out", [SEQ, D], F32, kind="ExternalOutput").ap()
    dk = "ExternalOutput" if debug else "Internal"
    xTd = nc.dram_tensor("xTd", [D, SEQ], BF16, kind=dk).ap()
    yTd = nc.dram_tensor("yTd", [D, SEQ], BF16, kind=dk).ap()
    xsc = [nc.dram_tensor(f"xsc{i}", [SEQ, D], F32).ap() for i in range(2)] if L > 1 else []
    spd = nc.dram_tensor("spd", [8, SEQ], F32).ap()
    cumd = nc.dram_tensor("cumd", [8, SEQ], F32, kind=dk).ap()
    totd = nc.dram_tensor("totd", [128, 1], F32).ap()
    offd = nc.dram_tensor("offd", [8, 16], F32).ap()
    wod = [nc.dram_tensor(f"wod{i}", [D, D], BF16).ap() for i in range(2)]

    with ExitStack() as st:
        def sb(name, shape, dt):
            return st.enter_context(nc.sbuf_tensor(name, shape, dt))

        S = Sched(nc, st)
        A = sb("A", [128, 65536], BF16)
        Wt = sb("Wt", [128, 2, KC, 512], BF16)
        KT = sb("KT", [128, SEQ], BF16)
        Vt = sb("Vt", [128, 32, 128], BF16)
        csts = sb("csts", [128, 272], F32)
        idb = sb("idb", [128, 128], BF16)
        onesb = sb("onesb", [128, 128], BF16)
        maskb = sb("maskb", [128, 128], BF16)
        wfg = sb("wfg", [128, KC, 8], BF16)
        bft = sb("bft", [128, 1], F32)
        csp_tok = sb("csp_tok", [128, 32, 8], F32)
        cwt = sb("cwt", [128, 4, 3], F32)
        pscale = sb("pscale", [128, 4], F32)
        pw = sb("pw", [128, 4, 128], BF16)
        stt = sb("stt", [128, 2, 24], F32)
        mv = sb("mv", [128, 2, 8], F32)
        dummy = sb("dummy_t", [128, 8], F32)
        C = sb("C", [128, 6656], F32)
        P = [st.enter_context(nc.psum_tensor(f"P{b}", [128, 512], F32)) for b in range(8)]

        xT = A[:, :].rearrange("p (kc t) -> p kc t", kc=KC)
        KORD = list(range(8, 16)) + list(range(8))

        def xt_reload(tt, half):
            k0 = half * 8
            S.op("sp", lambda e: e.dma_start(
                out=xT[:, k0:k0 + 8, tt * 512:(tt + 1) * 512],
                in_=xTd[k0 * 128:(k0 + 8) * 128, tt * 512:(tt + 1) * 512].rearrange("(kc p) t -> p kc t", p=128)),
                r=[("xTd", 4 * tt + i_) for i_ in range(4)], w=[k for kc in range(k0, k0 + 8) for k in xkeys(kc, tt * 512, 512)], dma=True)

        def akeys(off, n):
            return [("A", b) for b in range(off // 512, (off + n - 1) // 512 + 1)]

        def xkeys(kc, t0, n):
            return akeys(kc * SEQ + t0, n)

        def Af32(off, n_f32):
            return A[:, off:off + 2 * n_f32].bitcast(F32)

        def Cf(off, n):
            return C[:, off:off + n]

        def Cb(off, n_bf):
            return C[:, off:off + n_bf // 2].bitcast(BF16)

        cph = ("Cph",)

        def cbarrier():
            S.op("pool", lambda e: e.memset(dummy[:], 0.0), w=[cph])

        S.op("sp", lambda e: e.dma_start(out=csts[:], in_=cst), w=["csts"], dma=True)
        S.op("dve", lambda e: e.tensor_copy(out=idb[:], in_=csts[:, 0:128]), r=["csts"], w=["idb"])
        S.op("pool", lambda e: e.memset(onesb[:], 1.0), w=["onesb"])
        S.op("dve", lambda e: e.tensor_scalar(out=maskb[:], in0=csts[:, 128:256], scalar1=-1.0, scalar2=30000.0, op0=ALU.add, op1=ALU.mult),
             r=["csts"], w=["maskb"])
        invc = csts[:, 256:272]

        pbank = [0]

        XS = [0, 4096]
        RR = [8192, 12288]
        GBC, BBC, XB = 16384, 20480, 24576
        XTS = [26624, 28672]
        WO = 32768

        def emit_xT(t, src_f32, src_keys, i2, to_sbuf=False):
            if to_sbuf:
                xb = Cb(4096 + 1024 * (t % 2), 2048)
                xbk = [("C", "pxb", t % 2)]
                S.op("dve", lambda e: e.tensor_copy(out=xb, in_=src_f32), r=list(src_keys) + [cph], w=xbk)
            else:
                xb = A[:, XB:XB + 2048]
                xbk = akeys(XB, 2048)
                S.op("act", lambda e: e.activation(out=xb, in_=src_f32, func=AF.Copy), r=src_keys, w=xbk)
                xts_off = XTS[i2]
                xts = A[:, xts_off:xts_off + 2048].rearrange("p (kc t) -> p kc t", kc=KC)
            for half in range(2):
                b = 4 + 2 * (t % 2) + half
                Pb = P[b][:].bitcast(BF16)
                for j in range(8):
                    kc = half * 8 + j
                    S.op("pe", lambda e, kc=kc, j=j, Pb=Pb: e.transpose(out=Pb[:, j * 128:(j + 1) * 128],
                                                                       in_=xb[:, kc * 128:(kc + 1) * 128], identity=idb[:]),
                         r=xbk + ["idb"] + ([cph] if to_sbuf else []), w=[("P", b)])
                if to_sbuf:
                    S.op("act", lambda e, half=half, Pb=Pb: e.activation(
                        out=xT[:, half * 8:(half + 1) * 8, t * 128:(t + 1) * 128], in_=Pb[:, :].rearrange("p (j t) -> p j t", j=8),
                        func=AF.Copy),
                        r=[("P", b)], w=[k for kc in range(half * 8, half * 8 + 8) for k in xkeys(kc, t * 128, 128)])
                else:
                    S.op("act", lambda e, half=half, Pb=Pb: e.activation(
                        out=xts[:, half * 8:(half + 1) * 8, :], in_=Pb[:, :].rearrange("p (j t) -> p j t", j=8), func=AF.Copy),
                        r=[("P", b)], w=akeys(xts_off + half * 1024, 1024))
            if to_sbuf:
                return None
            return S.op("sp", lambda e: e.dma_start(out=xTd[:, t * 128:(t + 1) * 128].rearrange("(kc p) t -> p kc t", p=128), in_=xts),
                        r=akeys(xts_off, 2048), w=[("xTd", t)], dma=True)

        pxs = [Wt[:, 1, 0:8, :].rearrange("p a b -> p (a b)").bitcast(F32), Wt[:, 1, 8:16, :].rearrange("p a b -> p (a b)").bitcast(F32),
               Vt[:, :, :].rearrange("p a b -> p (a b)").bitcast(F32)]
        wall1 = [("W", 1, q_) for q_ in range(4)]
        pxw = [[("Wpx", 0)], [("Wpx", 1)], [("V", i_) for i_ in range(8)]]
        pxr = [wall1, wall1, []]

        def prologue_load(t):
            i3 = t % 3
            xs = pxs[i3]
            S.op("sp", lambda e: e.dma_start(out=xs, in_=x_in[t * 128:(t + 1) * 128, :]),
                 r=pxr[i3], w=pxw[i3], dma=True)

        def prologue_tile(t):
            i3 = t % 3
            if t == 0:
                for t_ in range(3):
                    prologue_load(t_)
            emit_xT(t, pxs[i3], pxw[i3] + pxr[i3], t % 2, to_sbuf=True)
            if t + 3 < 32:
                prologue_load(t + 3)

        out_ops = []
        for l in range(L):
            S.epoch = l
            x_cur = x_in if l == 0 else xsc[(l - 1) % 2]
            x_nxt = out if l == L - 1 else xsc[l % 2]
            win = w_in[l]

            def wload(slot, cols, l=l, win=win):
                o = 0
                for (c0, n) in cols:
                    S.op("pool", lambda e, o=o, c0=c0, n=n: e.dma_start(
                        out=Wt[:, slot, :, o:o + n], in_=win[:, c0:c0 + n].rearrange("(kc p) c -> p kc c", p=128)),
                        w=[("W", slot, o // 128)], dma=True)
                    o += n

            if l > 0:
                xt_reload(0, 0)
                xt_reload(1, 0)
                for tt in range(2, NT):
                    xt_reload(tt, 1)
                    xt_reload(tt, 0)

            groups = [("head", h) for h in range(8)] + [("conv", j) for j in range(4)] + [("pool", 0), ("pool", 2)]

            def gcols(g):
                kind, i = g
                if kind == "head":
                    return [(Q0 + i * 128, 128), (K0 + i * 128, 128), (V0 + i * 128, 128), (GA0 + i * 128, 128)]
                if kind == "conv":
                    return [(CB0 + i * 128, 128), (CC0 + i * 128, 128), (CH0 + i * 128, 128), (GC0 + i * 128, 128)]
                return [(PU0 + i * 128, 128), (GP0 + i * 128, 128), (PU0 + (i + 1) * 128, 128), (GP0 + (i + 1) * 128, 128)]

            def start_loads(lw):
                wn = w_in[lw]
                S.op("pool", lambda e: e.dma_start(out=wfg[:], in_=wn[:, FG0:FG0 + 8].rearrange("(kc p) c -> p kc c", p=128)),
                     w=["wfg"], dma=True)
                wload(0, gcols(("head", 0)), win=wn)

            if l == 0:
                start_loads(0)

            def nextbank():
                b = 5 + pbank[0] % 3
                pbank[0] += 1
                return b

            def proj_fm(slot, c0, tt, b):
                for ki, kc in enumerate(KORD):
                    S.op("pe", lambda e, kc=kc, ki=ki: e.matmul(P[b][:, :], lhsT=Wt[:, slot, kc, c0:c0 + 128], rhs=xT[:, kc, tt * 512:(tt + 1) * 512],
                                                                start=(ki == 0), stop=(ki == KC - 1)),
                         r=[("W", slot, c0 // 128)] + xkeys(kc, tt * 512, 512), w=[("P", b)])

            cbarrier()
            S.op("sp", lambda e, l=l: e.dma_start(out=bft[0:8, :], in_=b_f[l].rearrange("(h o) -> h o", o=1)), w=["bft"], dma=True)
            S.op("dve", lambda e: e.tensor_scalar(out=bft[0:8, :], in0=bft[0:8, :], scalar1=-1.0, scalar2=None, op0=ALU.mult),
                 r=["bft"], w=["bft"])
            for tt in range(NT):
                if l == 0:
                    for t_ in range(4 * tt, 4 * tt + 4):
                        prologue_tile(t_)
                b = 5 + tt % 3
                for ki, kc in enumerate(KORD):
                    S.op("pe", lambda e, kc=kc, ki=ki, tt=tt, b=b: e.matmul(P[b][0:8, :], lhsT=wfg[:, kc, :], rhs=xT[:, kc, tt * 512:(tt + 1) * 512],
                                                                           start=(ki == 0), stop=(ki == KC - 1)),
                         r=["wfg"] + xkeys(kc, tt * 512, 512), w=[("P", b)])
                eb = Cf((tt % 2) * 512, 512)
                ek = ("C", "e", tt % 2)
                S.op("act", lambda e, b=b, eb=eb: e.activation(out=eb[0:8, :], in_=P[b][0:8, :], func=AF.Exp, bias=bft[0:8, :], scale=-1.0),
                     r=[("P", b), "bft", cph], w=[ek])
                S.op("act", lambda e, eb=eb: e.activation(out=eb[0:8, :], in_=eb[0:8, :], func=AF.Ln, bias=1.0), r=[ek, cph], w=[ek])
                S.op("sp", lambda e, eb=eb, tt=tt: e.dma_start(out=spd[:, tt * 512:(tt + 1) * 512], in_=eb[0:8, :]),
                     r=[ek, cph], w=[("spd", tt)], dma=True)
                bk = nextbank()
                proj_fm(0, 128, tt, bk)
                S.op("act", lambda e, bk=bk, tt=tt: e.activation(out=KT[:, tt * 512:(tt + 1) * 512], in_=P[bk][:, :], func=AF.Copy),
                     r=[("P", bk)], w=[("KT", tt)])
            sp128 = Cf(1024, 256)
            c1 = Cf(1280, 256)
            ones = Cf(1536, 256)
            t8 = Cf(1792, 16)
            i8 = Cf(1808, 16)
            offs = Cf(1824, 1)
            S.op("sp", lambda e: e.dma_start(out=sp128, in_=spd.rearrange("h (s t) -> (h s) t", t=256)),
                 r=[("spd", tt) for tt in range(NT)] + [cph], w=[("C", "sp128")], dma=True)
            S.op("pool", lambda e: e.memset(ones, 1.0), r=[cph], w=[("C", "ones")])
            S.op("dve", lambda e: e.tensor_tensor_scan(out=c1, data0=ones, data1=sp128, initial=0.0, op0=ALU.mult, op1=ALU.add),
                 r=[("C", "sp128"), ("C", "ones"), cph], w=[("C", "c1")])
            S.op("sp", lambda e: e.dma_start(out=totd, in_=c1[:, 255:256]), r=[("C", "c1"), cph], w=["totd"], dma=True)
            S.op("sp", lambda e: e.dma_start(out=t8[0:8, :], in_=totd.rearrange("(h s) o -> h (s o)", s=16)),
                 r=["totd", cph], w=[("C", "t8")], dma=True)
            S.op("dve", lambda e: e.tensor_tensor_scan(out=i8[0:8, :], data0=ones[0:8, 0:16], data1=t8[0:8, :], initial=0.0,
                                                       op0=ALU.mult, op1=ALU.add),
                 r=[("C", "t8"), ("C", "ones"), cph], w=[("C", "i8")])
            S.op("dve", lambda e: e.tensor_tensor(out=i8[0:8, :], in0=i8[0:8, :], in1=t8[0:8, :], op=ALU.subtract),
                 r=[("C", "i8"), ("C", "t8"), cph], w=[("C", "i8")])
            S.op("sp", lambda e: e.dma_start(out=offd, in_=i8[0:8, :]), r=[("C", "i8"), cph], w=["offd"], dma=True)
            S.op("sp", lambda e: e.dma_start(out=offs, in_=offd.rearrange("h (s o) -> (h s) o", o=1)), r=["offd", cph], w=[("C", "offs")], dma=True)
            S.op("dve", lambda e: e.tensor_scalar(out=c1, in0=c1, scalar1=offs, scalar2=None, op0=ALU.add),
                 r=[("C", "c1"), ("C", "offs"), cph], w=[("C", "c1")])
            S.op("dve", lambda e: e.tensor_scalar(out=c1, in0=c1, scalar1=1.0 / SCALE, scalar2=None, op0=ALU.mult),
                 r=[("C", "c1"), cph], w=[("C", "c1")])
            S.op("sp", lambda e: e.dma_start(out=cumd.rearrange("h (s t) -> (h s) t", t=256), in_=c1), r=[("C", "c1"), cph], w=["cumd"], dma=True)

            csp2 = csp_tok[:, :, :].rearrange("p a b -> p (a b)")
            hb = Cb(2048, 256)
            cs_lo = Cf(2304, 256)
            lhb = Cb(2560, 256)
            llb = Cb(2816, 256)
            S.op("dve", lambda e, hb=hb, c1=c1: e.tensor_copy(out=hb, in_=c1), r=[("C", "c1"), cph], w=[("C", "hb")])
            S.op("dve", lambda e, hb=hb, c1=c1, cs_lo=cs_lo: e.tensor_tensor(out=cs_lo, in0=c1, in1=hb, op=ALU.subtract),
                 r=[("C", "c1"), ("C", "hb"), cph], w=[("C", "lo")])
            S.op("dve", lambda e, lhb=lhb, cs_lo=cs_lo: e.tensor_copy(out=lhb, in_=cs_lo), r=[("C", "lo"), cph], w=[("C", "lhb")])
            S.op("dve", lambda e, lhb=lhb, llb=llb, cs_lo=cs_lo: e.tensor_tensor(out=llb, in0=cs_lo, in1=lhb, op=ALU.subtract),
                 r=[("C", "lo"), ("C", "lhb"), cph], w=[("C", "llb")])
            bq = nextbank()
            for half in range(2):
                for pi, (part, pk) in enumerate(((hb, "hb"), (lhb, "lhb"), (llb, "llb"))):
                    S.op("pe", lambda e, half=half, part=part, pi=pi, bq=bq: e.matmul(
                        P[bq][:, half * 128:(half + 1) * 128], lhsT=part[:, half * 128:(half + 1) * 128], rhs=idb[:],
                        start=(pi == 0), stop=(pi == 2)),
                        r=[("C", pk), "idb", cph], w=[("P", bq)])
            S.op("act", lambda e, bq=bq, csp2=csp2: e.activation(out=csp2, in_=P[bq][:, 0:256], func=AF.Copy), r=[("P", bq)], w=["csp2"])
            cbarrier()

            QT = [Cb(0, 512), Cb(256, 512)]
            SG = [Cb(512, 512), Cb(768, 512)]
            CQ = [Cf(1024, 512), Cf(1536, 512)]
            TMP = [Cf(2048, 512), Cf(2560, 512), Cf(3072, 512)]
            PT = [Cb(3584, 512), Cb(3840, 512), Cb(4096, 512), Cb(4352, 512)]
            RL = Cf(4608, 512)
            OS = Cf(5120, 512)
            YO = [Cb(5632, 512), Cb(5888, 512)]
            VTS = [Cb(6144, 512), Cb(6400, 512)]
            TH = Cf(6144, 512)
            LA = 3

            wo = A[:, WO:WO + 32768].rearrange("p (kc n) -> p kc n", kc=KC)
            def woload(q4, l=l):
                S.op("sp", lambda e: e.dma_start(
                    out=wo[:, :, q4 * 512:(q4 + 1) * 512], in_=wod[l % 2][:, q4 * 512:(q4 + 1) * 512].rearrange("(kc p) n -> p kc n", p=128)),
                    r=[("wod", l % 2, r_) for r_ in range(4)],
                    w=[k for kc in range(KC) for k in akeys(WO + kc * 2048 + q4 * 512, 512)], dma=True)

            gbc = Af32(GBC, 2048)
            bbc = Af32(BBC, 2048)
            def yload(tg):
                s_ = tg % 2
                S.op("sp", lambda e: e.dma_start(out=Wt[:, s_, :, :], in_=yTd[:, tg * 512:(tg + 1) * 512].rearrange("(kc p) t -> p kc t", p=128)),
                     r=[("yTd", c_, tg) for c_ in range(16)], w=[("W", s_, q_) for q_ in range(4)], dma=True)

            def xload(t, x_cur=x_cur, l=l):
                off = XS[t % 2]
                S.op("sp", lambda e: e.dma_start(out=Af32(off, 2048), in_=x_cur[t * 128:(t + 1) * 128, :]),
                     r=[("xres", l, t)], w=akeys(off, 4096), dma=True)

            yo_cnt = [0]

            for gi, g in enumerate(groups):
                slot = gi % 2
                if gi + 1 < len(groups):
                    wload((gi + 1) % 2, gcols(groups[gi + 1]))
                if gi == 1:
                    for q4 in range(4):
                        S.op("pool", lambda e, q4=q4, l=l: e.dma_start(out=wod[l % 2][q4 * 512:(q4 + 1) * 512, :],
                                                                       in_=w_out[l, q4 * 512:(q4 + 1) * 512, :]),
                             w=[("wod", l % 2, q4)], dma=True)
                kind, gidx = g
                if kind == "head":
                    h = gidx
                    for tt in (range(NT) if h > 0 else ()):
                        b = nextbank()
                        proj_fm(slot, 128, tt, b)
                        S.op("act", lambda e, b=b, tt=tt: e.activation(out=KT[:, tt * 512:(tt + 1) * 512], in_=P[b][:, :], func=AF.Copy),
                             r=[("P", b)], w=[("KT", tt)])
                    def v_tr(tt):
                        b2 = nextbank()
                        Pb = P[b2][:].bitcast(BF16)
                        vts = VTS[tt % 2]
                        for j in range(4):
                            S.op("pe", lambda e, j=j: e.transpose(out=Pb[:, j * 128:(j + 1) * 128], in_=vts[:, j * 128:(j + 1) * 128],
                                                                  identity=idb[:]),
                                 r=[("C", "VTS", tt % 2), "idb", cph], w=[("P", b2)])
                        S.op("dve", lambda e: e.tensor_copy(
                            out=Vt[:, tt * 4:(tt + 1) * 4, :], in_=Pb[:, 0:512].rearrange("p (j d) -> p j d", j=4)),
                            r=[("P", b2)], w=[("V", tt)])

                    for tt in range(NT):
                        b = nextbank()
                        proj_fm(slot, 256, tt, b)
                        S.op("act", lambda e, b=b, tt=tt: e.activation(out=VTS[tt % 2], in_=P[b][:, :], func=AF.Copy),
                             r=[("P", b), cph], w=[("C", "VTS", tt % 2)])
                        if tt > 0:
                            v_tr(tt - 1)
                    v_tr(NT - 1)

                    def qg(qt, h=h, slot=slot):
                        i2 = qt % 2
                        b = nextbank()
                        proj_fm(slot, 0, qt, b)
                        S.op("act", lambda e, b=b: e.activation(out=QT[i2], in_=P[b][:, :], func=AF.Copy),
                             r=[("P", b), cph], w=[("C", "QT", i2)])
                        b2 = nextbank()
                        proj_fm(slot, 384, qt, b2)
                        S.op("act", lambda e, b2=b2: e.activation(out=TH, in_=P[b2][:, :], func=AF.Tanh, scale=0.5),
                             r=[("P", b2), cph], w=[("C", "VTS", 0), ("C", "VTS", 1)])
                        S.op("dve", lambda e, b2=b2: e.scalar_tensor_tensor(out=SG[i2], in0=TH, scalar=1.0, in1=P[b2][:, :],
                                                                            op0=ALU.add, op1=ALU.mult),
                             r=[("P", b2), ("C", "VTS", 0), ("C", "VTS", 1), cph], w=[("C", "SG", i2)])
                        S.op("sp", lambda e: e.dma_start(out=CQ[i2], in_=cumd[h:h + 1, qt * 512:(qt + 1) * 512].partition_broadcast(128)),
                             r=["cumd", cph], w=[("C", "CQ", i2)], dma=True)

                    qg(0)
                    for qt in range(NT):
                        i2 = qt % 2
                        nb = 4 * qt + 4

                        def front(i, qt=qt, i2=i2, h=h):
                            kt = i
                            c0 = max(0, kt - 4 * qt) * 128
                            sbk = i % 3
                            diag = kt >= 4 * qt
                            S.op("pe", lambda e: e.matmul(P[sbk][:, c0:512], lhsT=KT[:, kt * 128:(kt + 1) * 128], rhs=QT[i2][:, c0:512],
                                                          start=True, stop=not diag),
                                 r=[("KT", kt // 4), ("C", "QT", i2), cph], w=[("P", sbk)])
                            if diag:
                                S.op("pe", lambda e: e.matmul(P[sbk][:, c0:c0 + 128], lhsT=idb[:], rhs=maskb[:], start=False, stop=True),
                                     r=["idb", "maskb"], w=[("P", sbk)])
                            ckc = (kt % 2) * 128 + h * 16 + kt // 2
                            S.op("dve", lambda e: e.scalar_tensor_tensor(out=TMP[i % 3][:, c0:512], in0=P[sbk][:, c0:512],
                                                                         scalar=csp2[:, ckc:ckc + 1],
                                                                         in1=CQ[i2][:, c0:512], op0=ALU.add, op1=ALU.subtract),
                                 r=[("P", sbk), ("C", "CQ", i2), "csp2", cph], w=[("C", "TMP", i % 3)])
                            S.op("act", lambda e: e.activation(out=PT[i % 4][:, c0:512], in_=TMP[i % 3][:, c0:512], func=AF.Exp, scale=SCALE),
                                 r=[("C", "TMP", i % 3), cph], w=[("C", "PT", i % 4)])

                        def back(i, qt=qt, nb=nb):
                            kt = i
                            c0 = max(0, kt - 4 * qt) * 128
                            S.op("pe", lambda e: e.matmul(P[3][:, c0:512], lhsT=Vt[:, kt, :], rhs=PT[i % 4][:, c0:512],
                                                          start=(i == 0), stop=(i == nb - 1)),
                                 r=[("V", kt // 4), ("C", "PT", i % 4), cph], w=[("P", 3)])
                            S.op("pe", lambda e: e.matmul(P[4][:, c0:512], lhsT=onesb[:], rhs=PT[i % 4][:, c0:512],
                                                          start=(i == 0), stop=(i == nb - 1)),
                                 r=["onesb", ("C", "PT", i % 4), cph], w=[("P", 4)])

                        for i_ in range(min(LA, nb)):
                            front(i_)
                        if qt + 1 < NT:
                            qg(qt + 1)
                        for i in range(nb):
                            if i + LA < nb:
                                front(i + LA)
                            back(i)
                        yi = yo_cnt[0] % 2
                        yo_cnt[0] += 1
                        S.op("dve", lambda e: e.reciprocal(out=RL, in_=P[4][:, :]), r=[("P", 4), cph], w=[("C", "RL")])
                        S.op("dve", lambda e: e.scalar_tensor_tensor(out=OS, in0=P[3][:, :], scalar=0.5, in1=RL, op0=ALU.mult, op1=ALU.mult),
                             r=[("P", 3), ("C", "RL"), cph], w=[("C", "OS")])
                        S.op("pool", lambda e, yi=yi, i2=i2: e.tensor_tensor(out=YO[yi], in0=OS, in1=SG[i2], op=ALU.mult),
                             r=[("C", "OS"), ("C", "SG", i2), cph], w=[("C", "YO", yi)])
                        S.op("sp", lambda e, yi=yi, h=h, qt=qt: e.dma_start(out=yTd[h * 128:(h + 1) * 128, qt * 512:(qt + 1) * 512], in_=YO[yi]),
                             r=[("C", "YO", yi), cph], w=[("yTd", h, qt)], dma=True)
                    if h == 7:
                        cbarrier()

                elif kind == "conv":
                    j = gidx
                    if j == 0:
                        for k_ in range(3):
                            S.op("sp", lambda e, l=l, k_=k_: _cw_load(nc, e, cwt[:, :, k_], conv_w[l, k_]), w=[("cwt", k_)], dma=True)
                    HS = Cf(0, 512)
                    SGc = Cf(512, 512)
                    U = [Cf(1024, 514), Cf(1538, 514)]
                    Y = Cf(2052, 512)
                    YOc = [Cb(2564, 512), Cb(2820, 512)]
                    for tt in range(NT):
                        bs = [0, 1, 2, 3] if tt % 2 == 0 else [4, 5, 6, 7]
                        for q in range(4):
                            proj_fm(slot, q * 128, tt, bs[q])
                        u = U[tt % 2]
                        un = U[(tt + 1) % 2]
                        uk = ("C", "U", tt % 2)
                        unk = ("C", "U", (tt + 1) % 2)
                        if tt == 0:
                            S.op("pool", lambda e, u=u: e.memset(u[:, 0:2], 0.0), r=[cph], w=[("C", "Uh", 0)])
                        S.op("act", lambda e, b=bs[2]: e.activation(out=HS, in_=P[b][:, :], func=AF.Copy), r=[("P", bs[2]), cph], w=[("C", "HS")])
                        S.op("act", lambda e, b=bs[3]: e.activation(out=SGc, in_=P[b][:, :], func=AF.Silu), r=[("P", bs[3]), cph], w=[("C", "SGc")])
                        S.op("dve", lambda e, b=bs[1], u=u: e.tensor_tensor(out=u[:, 2:514], in0=P[b][:, :], in1=HS, op=ALU.mult),
                             r=[("P", bs[1]), ("C", "HS"), cph], w=[uk])
                        if tt + 1 < NT:
                            S.op("pool", lambda e, u=u, un=un: e.tensor_copy(out=un[:, 0:2], in_=u[:, 512:514]),
                                 r=[uk, cph], w=[("C", "Uh", (tt + 1) % 2)])
                        uhk = ("C", "Uh", tt % 2)
                        S.op("dve", lambda e, u=u, j=j: e.tensor_scalar(out=Y, in0=u[:, 2:514], scalar1=cwt[:, j, 2:3], scalar2=None, op0=ALU.mult),
                             r=[uk, ("cwt", 0), ("cwt", 1), ("cwt", 2), cph], w=[("C", "Y")])
                        S.op("dve", lambda e, u=u, j=j: e.scalar_tensor_tensor(out=Y, in0=u[:, 1:513], scalar=cwt[:, j, 1:2], in1=Y,
                                                                               op0=ALU.mult, op1=ALU.add),
                             r=[uk, uhk, ("cwt", 0), ("cwt", 1), ("cwt", 2), ("C", "Y"), cph], w=[("C", "Y")])
                        S.op("dve", lambda e, u=u, j=j: e.scalar_tensor_tensor(out=Y, in0=u[:, 0:512], scalar=cwt[:, j, 0:1], in1=Y,
                                                                               op0=ALU.mult, op1=ALU.add),
                             r=[uk, uhk, ("cwt", 0), ("cwt", 1), ("cwt", 2), ("C", "Y"), cph], w=[("C", "Y")])
                        S.op("dve", lambda e, b=bs[0]: e.tensor_tensor(out=Y, in0=P[b][:, :], in1=Y, op=ALU.mult),
                             r=[("P", bs[0]), ("C", "Y"), cph], w=[("C", "Y")])
                        yi = tt % 2
                        S.op("dve", lambda e, yi=yi: e.tensor_tensor(out=YOc[yi], in0=Y, in1=SGc, op=ALU.mult),
                             r=[("C", "Y"), ("C", "SGc"), cph], w=[("C", "YOc", yi)])
                        S.op("sp", lambda e, yi=yi, j=j, tt=tt: e.dma_start(
                            out=yTd[1024 + j * 128:1024 + (j + 1) * 128, tt * 512:(tt + 1) * 512], in_=YOc[yi]),
                            r=[("C", "YOc", yi), cph], w=[("yTd", 8 + j, tt)], dma=True)
                    if j == 3:
                        cbarrier()

                else:
                    g0 = gidx
                    if g0 == 0:
                        for gg in range(4):
                            S.op("pool", lambda e, gg=gg, l=l: e.dma_start(out=pw[:, gg, :], in_=pool_w[l, gg]), w=[("pw", gg)], dma=True)
                        S.op("sp", lambda e, l=l: _ps_load(nc, e, pscale, pool_scale[l]), w=["pscale"], dma=True)
                    U0 = [Cf(0, 528), Cf(528, 528)]
                    SA = Cf(1056, 528)
                    SB = Cf(1584, 528)
                    SGp2 = [Cf(2112, 512), Cf(3712, 512)]
                    ZB2 = [Cb(2624, 512), Cb(3456, 512)]
                    YOp = [Cb(2880, 512), Cb(3136, 512)]
                    TC = Cf(3392, 16)
                    pend = []
                    for gg in range(2):
                        g_ = g0 + gg
                        wdw = WINDOWS[g_]
                        nst = g_ + 1
                        for tt in range(NT):
                            idx = gg * NT + tt
                            bs = [0, 1, 2] if idx % 2 == 0 else [3, 4, 5]
                            proj_fm(slot, gg * 256, tt, bs[0])
                            proj_fm(slot, gg * 256 + 128, tt, bs[1])
                            if pend:
                                pend.pop()()
                            ib = idx % 2
                            SGp = SGp2[ib]
                            ZB = ZB2[ib]
                            sgk = ("C", "SGp", ib)
                            zbk = ("C", "ZB", ib)
                            u = U0[tt % 2]
                            un = U0[(tt + 1) % 2]
                            uk = ("C", "U0", tt % 2)
                            uhk = ("C", "U0h", tt % 2)
                            if tt == 0:
                                S.op("pool", lambda e, u=u: e.memset(u[:, 0:16], 0.0), r=[cph], w=[uhk])
                            S.op("act", lambda e, b=bs[0], u=u: e.activation(out=u[:, 16:528], in_=P[b][:, :], func=AF.Copy),
                                 r=[("P", bs[0]), cph], w=[uk])
                            S.op("act", lambda e, b=bs[1], SGp=SGp: e.activation(out=SGp, in_=P[b][:, :], func=AF.Silu),
                                 r=[("P", bs[1]), cph], w=[sgk])
                            if tt + 1 < NT:
                                S.op("pool", lambda e, u=u, un=un: e.tensor_copy(out=un[:, 0:16], in_=u[:, 512:528]),
                                     r=[uk, cph], w=[("C", "U0h", (tt + 1) % 2)])
                            src = u
                            srck = [uk, uhk]
                            bufs = [(SA, ("C", "SA")), (SB, ("C", "SB"))]
                            sh = 1
                            lo = 1
                            for s_ in range(nst):
                                dst, dk_ = bufs[s_ % 2]
                                S.op("dve", lambda e, dst=dst, src=src, lo=lo, sh=sh: e.tensor_tensor(
                                    out=dst[:, lo:528], in0=src[:, lo:528], in1=src[:, lo - sh:528 - sh], op=ALU.add),
                                    r=srck + [cph], w=[dk_])
                                src, srck = dst, [dk_]
                                sh *= 2
                                lo += sh
                            aw = src
                            S.op("dve", lambda e, aw=aw, u=u, wdw=wdw, ZB=ZB: e.scalar_tensor_tensor(
                                out=ZB, in0=aw[:, 16:528], scalar=1.0 / wdw, in1=u[:, 16:528], op0=ALU.mult, op1=ALU.subtract),
                                r=srck + [uk, cph], w=[zbk])
                            if tt == 0:
                                n1 = wdw - 1
                                S.op("dve", lambda e, aw=aw, n1=n1: e.tensor_tensor(out=TC[:, 0:n1], in0=aw[:, 16:16 + n1], in1=invc[:, 0:n1], op=ALU.mult),
                                     r=srck + ["csts", cph], w=[("C", "TC")])
                                S.op("dve", lambda e, u=u, n1=n1, ZB=ZB: e.tensor_tensor(out=ZB[:, 0:n1], in0=TC[:, 0:n1], in1=u[:, 16:16 + n1], op=ALU.subtract),
                                     r=[("C", "TC"), uk, zbk, cph], w=[zbk])

                            def tail(g_=g_, b2=bs[2], yi=idx % 2, tt=tt, ZB=ZB, SGp=SGp, zbk=zbk, sgk=sgk):
                                S.op("pe", lambda e: e.matmul(P[b2][:, :], lhsT=pw[:, g_, :], rhs=ZB, start=True, stop=True),
                                     r=[("pw", g_), zbk, cph], w=[("P", b2)])
                                S.op("dve", lambda e: e.scalar_tensor_tensor(
                                    out=YOp[yi], in0=P[b2][:, :], scalar=pscale[:, g_:g_ + 1], in1=SGp, op0=ALU.mult, op1=ALU.mult),
                                    r=[("P", b2), "pscale", sgk, cph], w=[("C", "YOp", yi)])
                                S.op("sp", lambda e: e.dma_start(
                                    out=yTd[1536 + g_ * 128:1536 + (g_ + 1) * 128, tt * 512:(tt + 1) * 512], in_=YOp[yi]),
                                    r=[("C", "YOp", yi), cph], w=[("yTd", 12 + g_, tt)], dma=True)
                                if g_ == 3:
                                    if tt == 0:
                                        yload(0)
                                    if tt >= 4:
                                        woload(tt - 4)
                            pend.append(tail)
                    if pend:
                        pend.pop()()
                    if g0 == 2:
                        cbarrier()

            xload(0)
            S.op("sp", lambda e, l=l: e.dma_start(out=gbc, in_=ln_g[l:l + 1, :].partition_broadcast(128)), w=akeys(GBC, 4096), dma=True)
            S.op("sp", lambda e, l=l: e.dma_start(out=bbc, in_=ln_b[l:l + 1, :].partition_broadcast(128)), w=akeys(BBC, 4096), dma=True)


            def ln_tail(t, l=l, x_nxt=x_nxt):
                r_off = RR[t % 2]
                rr = Af32(r_off, 2048)
                rk = akeys(r_off, 4096)
                p2 = t % 2
                mvp = mv[:, p2, :]
                sttp = stt[:, p2, :]
                S.op("dve", lambda e: e.bn_aggr(out=mvp[:, 0:2], in_=sttp), r=[("stt", p2, c) for c in range(4)], w=[("mv01", p2)])
                S.op("act", lambda e: e.activation(out=mvp[:, 2:3], in_=mvp[:, 1:2], func=AF.Sqrt, bias=LN_EPS), r=[("mv01", p2)], w=[("mv2", p2)])
                S.op("dve", lambda e: e.reciprocal(out=mvp[:, 3:4], in_=mvp[:, 2:3]), r=[("mv2", p2)], w=[("mv3", p2)])
                S.op("dve", lambda e: e.tensor_scalar(out=mvp[:, 4:5], in0=mvp[:, 0:1], scalar1=mvp[:, 3:4], scalar2=-1.0,
                                                      op0=ALU.mult, op1=ALU.mult), r=[("mv01", p2), ("mv3", p2)], w=[("mv4", p2)])
                S.op("act", lambda e: e.activation(out=rr, in_=rr, func=AF.Identity, bias=mvp[:, 4:5], scale=mvp[:, 3:4]),
                     r=rk + [("mv3", p2), ("mv4", p2)], w=rk)
                S.op("dve", lambda e: e.tensor_tensor(out=rr, in0=rr, in1=gbc, op=ALU.mult), r=rk + akeys(GBC, 4096), w=rk)
                S.op("pool", lambda e: e.tensor_tensor(out=rr, in0=rr, in1=bbc, op=ALU.add), r=rk + akeys(BBC, 4096), w=rk)
                so = S.op("sp", lambda e: e.dma_start(out=x_nxt[t * 128:(t + 1) * 128, :], in_=rr),
                          r=rk, w=[("xres", l + 1, t)], dma=True)
                if l == L - 1:
                    out_ops.append(so)
                else:
                    emit_xT(t, rr, rk, t % 2)

            for tg in range(NT):
                if tg + 1 < NT:
                    yload(tg + 1)
                if tg == NT - 1 and l + 1 < L:
                    start_loads(l + 1)
                ys = tg % 2
                for ts in range(4):
                    t = tg * 4 + ts
                    if t + 1 < 32:
                        xload(t + 1)
                    xs_off = XS[t % 2]
                    xs = Af32(xs_off, 2048)
                    r_off = RR[t % 2]
                    rr = Af32(r_off, 2048)
                    for cg in range(4):
                        for kc in range(KC):
                            S.op("pe", lambda e, kc=kc, cg=cg, ys=ys, ts=ts: e.matmul(
                                P[cg][:, :], lhsT=Wt[:, ys, kc, ts * 128:(ts + 1) * 128], rhs=wo[:, kc, cg * 512:(cg + 1) * 512],
                                start=(kc == 0), stop=(kc == KC - 1)),
                                r=[("W", ys, q_) for q_ in range(4)] + akeys(WO + kc * 2048 + cg * 512, 512), w=[("P", cg)])
                    if t > 0:
                        ln_tail(t - 1)
                    for cg in range(4):
                        S.op("dve", lambda e, cg=cg, xs=xs, rr=rr: e.scalar_tensor_tensor(
                            out=rr[:, cg * 512:(cg + 1) * 512], in0=xs[:, cg * 512:(cg + 1) * 512], scalar=ALPHA, in1=P[cg][:, :],
                            op0=ALU.mult, op1=ALU.add),
                            r=[("P", cg)] + akeys(xs_off + cg * 1024, 1024), w=akeys(r_off + cg * 1024, 1024))
                        S.op("dve", lambda e, cg=cg, rr=rr, t=t: e.bn_stats(out=stt[:, t % 2, cg * 6:(cg + 1) * 6], in_=rr[:, cg * 512:(cg + 1) * 512]),
                             r=akeys(r_off + cg * 1024, 1024), w=[("stt", t % 2, cg)])
            if l < L - 1:
                for tt in range(2):
                    xt_reload(tt, 1)
            ln_tail(31)

        if debug:
            out_ops = [o for o in S.ops if o.dma]
        S.op("sp", None, extra=out_ops)
        S.run()
    return nc


def _cw_load(nc, e, cwt, src):
    with nc.allow_non_contiguous_dma(reason="tiny per-channel taps"):
        return e.dma_start(out=cwt, in_=src.rearrange("(j p) -> p j", p=128))


def _ps_load(nc, e, pscale, src):
    with nc.allow_non_contiguous_dma(reason="tiny per-channel scale"):
        return e.dma_start(out=pscale[:, :], in_=src.rearrange("(g p) -> p g", p=128))


def _consts():
    c = np.zeros((128, 272), np.float32)
    c[:, 0:128] = np.eye(128, dtype=np.float32)
    c[:, 128:256] = np.triu(np.ones((128, 128), np.float32))
    c[:, 256:272] = 1.0 / np.arange(1, 17, dtype=np.float32)[None, :]
    return c


_NC_CACHE = {}


def _get_nc(L):
    if L not in _NC_CACHE:
        _NC_CACHE[L] = build_nc(L)
    return _NC_CACHE[L]


def kernel(x, w_in, b_f, conv_w, pool_w, pool_scale, w_out, ln_g, ln_b):
    f = lambda a: np.ascontiguousarray(np.asarray(a, dtype=np.float32))
    x, w_in, b_f, conv_w, pool_w, pool_scale, w_out, ln_g, ln_b = map(
        f, (x, w_in, b_f, conv_w, pool_w, pool_scale, w_out, ln_g, ln_b))
    cst = _consts()
    n = 8
    if FUSED:
        nc = _get_nc(DEPTH)
        in_maps = [{"x": x[c], "w_in": w_in, "b_f": b_f, "conv_w": conv_w, "pool_w": pool_w, "pool_scale": pool_scale,
                    "w_out": w_out, "ln_g": ln_g, "ln_b": ln_b, "cst": cst} for c in range(n)]
        res = run_bass_kernel_spmd(nc, in_maps, core_ids=list(range(n)))
        return np.stack([res.results[c]["out"] for c in range(n)], axis=0)
    nc = _get_nc(1)
    cur = [x[c] for c in range(n)]
    for l in range(DEPTH):
        in_maps = [{"x": cur[c], "w_in": w_in[l:l + 1], "b_f": b_f[l:l + 1], "conv_w": conv_w[l:l + 1], "pool_w": pool_w[l:l + 1],
                    "pool_scale": pool_scale[l:l + 1], "w_out": w_out[l:l + 1], "ln_g": ln_g[l:l + 1], "ln_b": ln_b[l:l + 1],
                    "cst": cst} for c in range(n)]
        res = run_bass_kernel_spmd(nc, in_maps, core_ids=list(range(n)))
        cur = [np.asarray(res.results[c]["out"]) for c in range(n)]
    return np.stack(cur, axis=0)
```

```python
import numpy as np
from contextlib import ExitStack
import concourse.bass as bass
import concourse.mybir as mybir
from concourse.bass_utils import run_bass_kernel_spmd

F32 = mybir.dt.float32
BF16 = mybir.dt.bfloat16
ALU = mybir.AluOpType
AF = mybir.ActivationFunctionType

FUSED = True
DEPTH = 4
D = 2048
SEQ = 4096
KC = 16
NT = 8
IN_W = 7176
Q0, K0, V0, GA0, FG0, CB0, CC0, CH0, GC0, PU0, GP0 = 0, 1024, 2048, 3072, 4096, 4104, 4616, 5128, 5640, 6152, 6664
SCALE = 128 ** -0.5
ALPHA = (2 * DEPTH) ** 0.25
LN_EPS = 1e-5
WINDOWS = (2, 4, 8, 16)


class Op:
    __slots__ = ("eng", "fn", "deps", "dma", "sem", "val", "sig", "epoch")


class Sched:
    def __init__(self, nc, stack, n_dma_sems=None):
        self.nc = nc
        self.stack = stack
        self.ops = []
        self.lastw = {}
        self.rd_c = {}
        self.rd_d = {}
        self.epoch = 0
        self.prog = {}
        nd = n_dma_sems or {"sp": 12, "pool": 6}
        self.dpool = {q: [[stack.enter_context(nc.semaphore(f"d_{q}{i}")), 0, None] for i in range(n)]
                      for q, n in nd.items()}
        self.dnext = {q: 0 for q in nd}

    def _prog(self, eng, epoch):
        k = (eng, epoch)
        if k not in self.prog:
            self.prog[k] = [self.stack.enter_context(self.nc.semaphore(f"p_{eng}{epoch}")), 0]
        return self.prog[k]

    def op(self, eng, fn, r=(), w=(), dma=False, extra=()):
        o = Op()
        o.eng, o.fn, o.dma, o.sig, o.epoch = eng, fn, dma, False, self.epoch
        o.sem = None
        o.val = 0
        deps = {}

        def add(d, kind):
            if d is None or d is o:
                return
            if (not dma) and (not d.dma) and d.eng == eng:
                if eng == "pe" or kind != "raw":
                    return
            deps[id(d)] = d

        for k in r:
            add(self.lastw.get(k), "raw")
        for k in w:
            add(self.lastw.get(k), "waw")
            for d in self.rd_c.get(k, {}).values():
                add(d, "war")
            for d in self.rd_d.get(k, ()):
                add(d, "war")
        for d in extra:
            add(d, "raw")
        if dma:
            slot = self.dpool[eng][self.dnext[eng] % len(self.dpool[eng])]
            self.dnext[eng] += 1
            if slot[2] is not None:
                deps[id(slot[2])] = slot[2]
            slot[1] += 1
            slot[2] = o
            o.sem = slot[0]
            o.val = 16 * slot[1]
        o.deps = list(deps.values())
        for k in w:
            self.lastw[k] = o
            self.rd_c[k] = {}
            self.rd_d[k] = []
        for k in r:
            if dma:
                self.rd_d.setdefault(k, []).append(o)
            else:
                self.rd_c.setdefault(k, {})[eng] = o
        self.ops.append(o)
        return o

    def finalize(self):
        for o in self.ops:
            for d in o.deps:
                d.sig = True
        for o in self.ops:
            if not o.dma and o.sig:
                p = self._prog(o.eng, o.epoch)
                p[1] += 1
                o.sem = p[0]
                o.val = p[1]

    def emit(self, engname, eng):
        seen = {}
        for o in self.ops:
            if o.eng != engname:
                continue
            for d in o.deps:
                if seen.get(id(d.sem), 0) < d.val:
                    eng.wait_ge(d.sem, d.val)
                    seen[id(d.sem)] = d.val
            if o.fn is None:
                continue
            ins = o.fn(eng)
            if o.dma:
                ins.then_inc(o.sem, 16)
            elif o.sig:
                ins.then_inc(o.sem, 1)

    def run(self):
        self.finalize()
        with self.nc.Block() as block:
            @block.tensor
            def _(e):
                self.emit("pe", e)

            @block.scalar
            def _(e):
                self.emit("act", e)

            @block.vector
            def _(e):
                self.emit("dve", e)

            @block.gpsimd
            def _(e):
                self.emit("pool", e)

            @block.sync
            def _(e):
                self.emit("sp", e)


def build_nc(L, debug=False):
    nc = bass.Bass("TRN2", target_bir_lowering=False)
    x_in = nc.dram_tensor("x", [SEQ, D], F32, kind="ExternalInput").ap()
    w_in = nc.dram_tensor("w_in", [L, D, IN_W], F32, kind="ExternalInput").ap()
    b_f = nc.dram_tensor("b_f", [L, 8], F32, kind="ExternalInput").ap()
    conv_w = nc.dram_tensor("conv_w", [L, 3, 512], F32, kind="ExternalInput").ap()
    pool_w = nc.dram_tensor("pool_w", [L, 4, 128, 128], F32, kind="ExternalInput").ap()
    pool_scale = nc.dram_tensor("pool_scale", [L, 512], F32, kind="ExternalInput").ap()
    w_out = nc.dram_tensor("w_out", [L, D, D], F32, kind="ExternalInput").ap()
    ln_g = nc.dram_tensor("ln_g", [L, D], F32, kind="ExternalInput").ap()
    ln_b = nc.dram_tensor("ln_b", [L, D], F32, kind="ExternalInput").ap()
    cst = nc.dram_tensor("cst", [128, 272], F32, kind="ExternalInput").ap()
    out = nc.dram_tensor("out", [SEQ, D], F32, kind="ExternalOutput").ap()
    dk = "ExternalOutput" if debug else "Internal"
    xTd = nc.dram_tensor("xTd", [D, SEQ], BF16, kind=dk).ap()
    yTd = nc.dram_tensor("yTd", [D, SEQ], BF16, kind=dk).ap()
    xsc = [nc.dram_tensor(f"xsc{i}", [SEQ, D], F32).ap() for i in range(2)] if L > 1 else []
    spd = nc.dram_tensor("spd", [8, SEQ], F32).ap()
    cumd = nc.dram_tensor("cumd", [8, SEQ], F32, kind=dk).ap()
    totd = nc.dram_tensor("totd", [128, 1], F32).ap()
    offd = nc.dram_tensor("offd", [8, 16], F32).ap()
    wod = [nc.dram_tensor(f"wod{i}", [D, D], BF16).ap() for i in range(2)]

    with ExitStack() as st:
        def sb(name, shape, dt):
            return st.enter_context(nc.sbuf_tensor(name, shape, dt))

        S = Sched(nc, st)
        A = sb("A", [128, 65536], BF16)
        Wt = sb("Wt", [128, 2, KC, 512], BF16)
        KT = sb("KT", [128, SEQ], BF16)
        Vt = sb("Vt", [128, 32, 128], BF16)
        csts = sb("csts", [128, 272], F32)
        idb = sb("idb", [128, 128], BF16)
        onesb = sb("onesb", [128, 128], BF16)
        maskb = sb("maskb", [128, 128], BF16)
        wfg = sb("wfg", [128, KC, 8], BF16)
        bft = sb("bft", [128, 1], F32)
        csp_tok = sb("csp_tok", [128, 32, 8], F32)
        cwt = sb("cwt", [128, 4, 3], F32)
        pscale = sb("pscale", [128, 4], F32)
        pw = sb("pw", [128, 4, 128], BF16)
        stt = sb("stt", [128, 2, 24], F32)
        mv = sb("mv", [128, 2, 8], F32)
        dummy = sb("dummy_t", [128, 8], F32)
        C = sb("C", [128, 6656], F32)
        P = [st.enter_context(nc.psum_tensor(f"P{b}", [128, 512], F32)) for b in range(8)]

        xT = A[:, :].rearrange("p (kc t) -> p kc t", kc=KC)
        KORD = list(range(8, 16)) + list(range(8))

        def xt_reload(tt, half):
            k0 = half * 8
            S.op("sp", lambda e: e.dma_start(
                out=xT[:, k0:k0 + 8, tt * 512:(tt + 1) * 512],
                in_=xTd[k0 * 128:(k0 + 8) * 128, tt * 512:(tt + 1) * 512].rearrange("(kc p) t -> p kc t", p=128)),
                r=[("xTd", 4 * tt + i_) for i_ in range(4)], w=[k for kc in range(k0, k0 + 8) for k in xkeys(kc, tt * 512, 512)], dma=True)

        def akeys(off, n):
            return [("A", b) for b in range(off // 512, (off + n - 1) // 512 + 1)]

        def xkeys(kc, t0, n):
            return akeys(kc * SEQ + t0, n)

        def Af32(off, n_f32):
            return A[:, off:off + 2 * n_f32].bitcast(F32)

        def Cf(off, n):
            return C[:, off:off + n]

        def Cb(off, n_bf):
            return C[:, off:off + n_bf // 2].bitcast(BF16)

        cph = ("Cph",)

        def cbarrier():
            S.op("pool", lambda e: e.memset(dummy[:], 0.0), w=[cph])

        S.op("sp", lambda e: e.dma_start(out=csts[:], in_=cst), w=["csts"], dma=True)
        S.op("dve", lambda e: e.tensor_copy(out=idb[:], in_=csts[:, 0:128]), r=["csts"], w=["idb"])
        S.op("pool", lambda e: e.memset(onesb[:], 1.0), w=["onesb"])
        S.op("dve", lambda e: e.tensor_scalar(out=maskb[:], in0=csts[:, 128:256], scalar1=-1.0, scalar2=30000.0, op0=ALU.add, op1=ALU.mult),
             r=["csts"], w=["maskb"])
        invc = csts[:, 256:272]

        pbank = [0]

        XS = [0, 4096]
        RR = [8192, 12288]
        GBC, BBC, XB = 16384, 20480, 24576
        XTS = [26624, 28672]
        WO = 32768

        def emit_xT(t, src_f32, src_keys, i2, to_sbuf=False):
            if to_sbuf:
                xb = Cb(4096 + 1024 * (t % 2), 2048)
                xbk = [("C", "pxb", t % 2)]
                S.op("dve", lambda e: e.tensor_copy(out=xb, in_=src_f32), r=list(src_keys) + [cph], w=xbk)
            else:
                xb = A[:, XB:XB + 2048]
                xbk = akeys(XB, 2048)
                S.op("act", lambda e: e.activation(out=xb, in_=src_f32, func=AF.Copy), r=src_keys, w=xbk)
                xts_off = XTS[i2]
                xts = A[:, xts_off:xts_off + 2048].rearrange("p (kc t) -> p kc t", kc=KC)
            for half in range(2):
                b = 4 + 2 * (t % 2) + half
                Pb = P[b][:].bitcast(BF16)
                for j in range(8):
                    kc = half * 8 + j
                    S.op("pe", lambda e, kc=kc, j=j, Pb=Pb: e.transpose(out=Pb[:, j * 128:(j + 1) * 128],
                                                                       in_=xb[:, kc * 128:(kc + 1) * 128], identity=idb[:]),
                         r=xbk + ["idb"] + ([cph] if to_sbuf else []), w=[("P", b)])
                if to_sbuf:
                    S.op("act", lambda e, half=half, Pb=Pb: e.activation(
                        out=xT[:, half * 8:(half + 1) * 8, t * 128:(t + 1) * 128], in_=Pb[:, :].rearrange("p (j t) -> p j t", j=8),
                        func=AF.Copy),
                        r=[("P", b)], w=[k for kc in range(half * 8, half * 8 + 8) for k in xkeys(kc, t * 128, 128)])
                else:
                    S.op("act", lambda e, half=half, Pb=Pb: e.activation(
                        out=xts[:, half * 8:(half + 1) * 8, :], in_=Pb[:, :].rearrange("p (j t) -> p j t", j=8), func=AF.Copy),
                        r=[("P", b)], w=akeys(xts_off + half * 1024, 1024))
            if to_sbuf:
                return None
            return S.op("sp", lambda e: e.dma_start(out=xTd[:, t * 128:(t + 1) * 128].rearrange("(kc p) t -> p kc t", p=128), in_=xts),
                        r=akeys(xts_off, 2048), w=[("xTd", t)], dma=True)

        pxs = [Wt[:, 1, 0:8, :].rearrange("p a b -> p (a b)").bitcast(F32), Wt[:, 1, 8:16, :].rearrange("p a b -> p (a b)").bitcast(F32),
               Vt[:, :, :].rearrange("p a b -> p (a b)").bitcast(F32)]
        wall1 = [("W", 1, q_) for q_ in range(4)]
        pxw = [[("Wpx", 0)], [("Wpx", 1)], [("V", i_) for i_ in range(8)]]
        pxr = [wall1, wall1, []]

        def prologue_load(t):
            i3 = t % 3
            xs = pxs[i3]
            S.op("sp", lambda e: e.dma_start(out=xs, in_=x_in[t * 128:(t + 1) * 128, :]),
                 r=pxr[i3], w=pxw[i3], dma=True)

        def prologue_tile(t):
            i3 = t % 3
            if t == 0:
                for t_ in range(3):
                    prologue_load(t_)
            emit_xT(t, pxs[i3], pxw[i3] + pxr[i3], t % 2, to_sbuf=True)
            if t + 3 < 32:
                prologue_load(t + 3)

        out_ops = []
        for l in range(L):
            S.epoch = l
            x_cur = x_in if l == 0 else xsc[(l - 1) % 2]
            x_nxt = out if l == L - 1 else xsc[l % 2]
            win = w_in[l]

            def wload(slot, cols, l=l, win=win):
                o = 0
                for (c0, n) in cols:
                    S.op("pool", lambda e, o=o, c0=c0, n=n: e.dma_start(
                        out=Wt[:, slot, :, o:o + n], in_=win[:, c0:c0 + n].rearrange("(kc p) c -> p kc c", p=128)),
                        w=[("W", slot, o // 128)], dma=True)
                    o += n

            if l > 0:
                xt_reload(0, 0)
                xt_reload(1, 0)
                for tt in range(2, NT):
                    xt_reload(tt, 1)
                    xt_reload(tt, 0)

            groups = [("head", h) for h in range(8)] + [("conv", j) for j in range(4)] + [("pool", 0), ("pool", 2)]

            def gcols(g):
                kind, i = g
                if kind == "head":
                    return [(Q0 + i * 128, 128), (K0 + i * 128, 128), (V0 + i * 128, 128), (GA0 + i * 128, 128)]
                if kind == "conv":
                    return [(CB0 + i * 128, 128), (CC0 + i * 128, 128), (CH0 + i * 128, 128), (GC0 + i * 128, 128)]
                return [(PU0 + i * 128, 128), (GP0 + i * 128, 128), (PU0 + (i + 1) * 128, 128), (GP0 + (i + 1) * 128, 128)]

            def start_loads(lw):
                wn = w_in[lw]
                S.op("pool", lambda e: e.dma_start(out=wfg[:], in_=wn[:, FG0:FG0 + 8].rearrange("(kc p) c -> p kc c", p=128)),
                     w=["wfg"], dma=True)
                wload(0, gcols(("head", 0)), win=wn)

            if l == 0:
                start_loads(0)

            def nextbank():
                b = 5 + pbank[0] % 3
                pbank[0] += 1
                return b

            def proj_fm(slot, c0, tt, b):
                for ki, kc in enumerate(KORD):
                    S.op("pe", lambda e, kc=kc, ki=ki: e.matmul(P[b][:, :], lhsT=Wt[:, slot, kc, c0:c0 + 128], rhs=xT[:, kc, tt * 512:(tt + 1) * 512],
                                                                start=(ki == 0), stop=(ki == KC - 1)),
                         r=[("W", slot, c0 // 128)] + xkeys(kc, tt * 512, 512), w=[("P", b)])

            cbarrier()
            S.op("sp", lambda e, l=l: e.dma_start(out=bft[0:8, :], in_=b_f[l].rearrange("(h o) -> h o", o=1)), w=["bft"], dma=True)
            S.op("dve", lambda e: e.tensor_scalar(out=bft[0:8, :], in0=bft[0:8, :], scalar1=-1.0, scalar2=None, op0=ALU.mult),
                 r=["bft"], w=["bft"])
            for tt in range(NT):
                if l == 0:
                    for t_ in range(4 * tt, 4 * tt + 4):
                        prologue_tile(t_)
                b = 5 + tt % 3
                for ki, kc in enumerate(KORD):
                    S.op("pe", lambda e, kc=kc, ki=ki, tt=tt, b=b: e.matmul(P[b][0:8, :], lhsT=wfg[:, kc, :], rhs=xT[:, kc, tt * 512:(tt + 1) * 512],
                                                                           start=(ki == 0), stop=(ki == KC - 1)),
                         r=["wfg"] + xkeys(kc, tt * 512, 512), w=[("P", b)])
                eb = Cf((tt % 2) * 512, 512)
                ek = ("C", "e", tt % 2)
                S.op("act", lambda e, b=b, eb=eb: e.activation(out=eb[0:8, :], in_=P[b][0:8, :], func=AF.Exp, bias=bft[0:8, :], scale=-1.0),
                     r=[("P", b), "bft", cph], w=[ek])
                S.op("act", lambda e, eb=eb: e.activation(out=eb[0:8, :], in_=eb[0:8, :], func=AF.Ln, bias=1.0), r=[ek, cph], w=[ek])
                S.op("sp", lambda e, eb=eb, tt=tt: e.dma_start(out=spd[:, tt * 512:(tt + 1) * 512], in_=eb[0:8, :]),
                     r=[ek, cph], w=[("spd", tt)], dma=True)
                bk = nextbank()
                proj_fm(0, 128, tt, bk)
                S.op("act", lambda e, bk=bk, tt=tt: e.activation(out=KT[:, tt * 512:(tt + 1) * 512], in_=P[bk][:, :], func=AF.Copy),
                     r=[("P", bk)], w=[("KT", tt)])
            sp128 = Cf(1024, 256)
            c1 = Cf(1280, 256)
            ones = Cf(1536, 256)
            t8 = Cf(1792, 16)
            i8 = Cf(1808, 16)
            offs = Cf(1824, 1)
            S.op("sp", lambda e: e.dma_start(out=sp128, in_=spd.rearrange("h (s t) -> (h s) t", t=256)),
                 r=[("spd", tt) for tt in range(NT)] + [cph], w=[("C", "sp128")], dma=True)
            S.op("pool", lambda e: e.memset(ones, 1.0), r=[cph], w=[("C", "ones")])
            S.op("dve", lambda e: e.tensor_tensor_scan(out=c1, data0=ones, data1=sp128, initial=0.0, op0=ALU.mult, op1=ALU.add),
                 r=[("C", "sp128"), ("C", "ones"), cph], w=[("C", "c1")])
            S.op("sp", lambda e: e.dma_start(out=totd, in_=c1[:, 255:256]), r=[("C", "c1"), cph], w=["totd"], dma=True)
            S.op("sp", lambda e: e.dma_start(out=t8[0:8, :], in_=totd.rearrange("(h s) o -> h (s o)", s=16)),
                 r=["totd", cph], w=[("C", "t8")], dma=True)
            S.op("dve", lambda e: e.tensor_tensor_scan(out=i8[0:8, :], data0=ones[0:8, 0:16], data1=t8[0:8, :], initial=0.0,
                                                       op0=ALU.mult, op1=ALU.add),
                 r=[("C", "t8"), ("C", "ones"), cph], w=[("C", "i8")])
            S.op("dve", lambda e: e.tensor_tensor(out=i8[0:8, :], in0=i8[0:8, :], in1=t8[0:8, :], op=ALU.subtract),
                 r=[("C", "i8"), ("C", "t8"), cph], w=[("C", "i8")])
            S.op("sp", lambda e: e.dma_start(out=offd, in_=i8[0:8, :]), r=[("C", "i8"), cph], w=["offd"], dma=True)
            S.op("sp", lambda e: e.dma_start(out=offs, in_=offd.rearrange("h (s o) -> (h s) o", o=1)), r=["offd", cph], w=[("C", "offs")], dma=True)
            S.op("dve", lambda e: e.tensor_scalar(out=c1, in0=c1, scalar1=offs, scalar2=None, op0=ALU.add),
                 r=[("C", "c1"), ("C", "offs"), cph], w=[("C", "c1")])
            S.op("dve", lambda e: e.tensor_scalar(out=c1, in0=c1, scalar1=1.0 / SCALE, scalar2=None, op0=ALU.mult),
                 r=[("C", "c1"), cph], w=[("C", "c1")])
            S.op("sp", lambda e: e.dma_start(out=cumd.rearrange("h (s t) -> (h s) t", t=256), in_=c1), r=[("C", "c1"), cph], w=["cumd"], dma=True)

            csp2 = csp_tok[:, :, :].rearrange("p a b -> p (a b)")
            hb = Cb(2048, 256)
            cs_lo = Cf(2304, 256)
            lhb = Cb(2560, 256)
            llb = Cb(2816, 256)
            S.op("dve", lambda e, hb=hb, c1=c1: e.tensor_copy(out=hb, in_=c1), r=[("C", "c1"), cph], w=[("C", "hb")])
            S.op("dve", lambda e, hb=hb, c1=c1, cs_lo=cs_lo: e.tensor_tensor(out=cs_lo, in0=c1, in1=hb, op=ALU.subtract),
                 r=[("C", "c1"), ("C", "hb"), cph], w=[("C", "lo")])
            S.op("dve", lambda e, lhb=lhb, cs_lo=cs_lo: e.tensor_copy(out=lhb, in_=cs_lo), r=[("C", "lo"), cph], w=[("C", "lhb")])
            S.op("dve", lambda e, lhb=lhb, llb=llb, cs_lo=cs_lo: e.tensor_tensor(out=llb, in0=cs_lo, in1=lhb, op=ALU.subtract),
                 r=[("C", "lo"), ("C", "lhb"), cph], w=[("C", "llb")])
            bq = nextbank()
            for half in range(2):
                for pi, (part, pk) in enumerate(((hb, "hb"), (lhb, "lhb"), (llb, "llb"))):
                    S.op("pe", lambda e, half=half, part=part, pi=pi, bq=bq: e.matmul(
                        P[bq][:, half * 128:(half + 1) * 128], lhsT=part[:, half * 128:(half + 1) * 128], rhs=idb[:],
                        start=(pi == 0), stop=(pi == 2)),
                        r=[("C", pk), "idb", cph], w=[("P", bq)])
            S.op("act", lambda e, bq=bq, csp2=csp2: e.activation(out=csp2, in_=P[bq][:, 0:256], func=AF.Copy), r=[("P", bq)], w=["csp2"])
            cbarrier()

            QT = [Cb(0, 512), Cb(256, 512)]
            SG = [Cb(512, 512), Cb(768, 512)]
            CQ = [Cf(1024, 512), Cf(1536, 512)]
            TMP = [Cf(2048, 512), Cf(2560, 512), Cf(3072, 512)]
            PT = [Cb(3584, 512), Cb(3840, 512), Cb(4096, 512), Cb(4352, 512)]
            RL = Cf(4608, 512)
            OS = Cf(5120, 512)
            YO = [Cb(5632, 512), Cb(5888, 512)]
            VTS = [Cb(6144, 512), Cb(6400, 512)]
            TH = Cf(6144, 512)
            LA = 3

            wo = A[:, WO:WO + 32768].rearrange("p (kc n) -> p kc n", kc=KC)
            def woload(q4, l=l):
                S.op("sp", lambda e: e.dma_start(
                    out=wo[:, :, q4 * 512:(q4 + 1) * 512], in_=wod[l % 2][:, q4 * 512:(q4 + 1) * 512].rearrange("(kc p) n -> p kc n", p=128)),
                    r=[("wod", l % 2, r_) for r_ in range(4)],
                    w=[k for kc in range(KC) for k in akeys(WO + kc * 2048 + q4 * 512, 512)], dma=True)

            gbc = Af32(GBC, 2048)
            bbc = Af32(BBC, 2048)
            def yload(tg):
                s_ = tg % 2
                S.op("sp", lambda e: e.dma_start(out=Wt[:, s_, :, :], in_=yTd[:, tg * 512:(tg + 1) * 512].rearrange("(kc p) t -> p kc t", p=128)),
                     r=[("yTd", c_, tg) for c_ in range(16)], w=[("W", s_, q_) for q_ in range(4)], dma=True)

            def xload(t, x_cur=x_cur, l=l):
                off = XS[t % 2]
                S.op("sp", lambda e: e.dma_start(out=Af32(off, 2048), in_=x_cur[t * 128:(t + 1) * 128, :]),
                     r=[("xres", l, t)], w=akeys(off, 4096), dma=True)

            yo_cnt = [0]

            for gi, g in enumerate(groups):
                slot = gi % 2
                if gi + 1 < len(groups):
                    wload((gi + 1) % 2, gcols(groups[gi + 1]))
                if gi == 1:
                    for q4 in range(4):
                        S.op("pool", lambda e, q4=q4, l=l: e.dma_start(out=wod[l % 2][q4 * 512:(q4 + 1) * 512, :],
                                                                       in_=w_out[l, q4 * 512:(q4 + 1) * 512, :]),
                             w=[("wod", l % 2, q4)], dma=True)
                kind, gidx = g
                if kind == "head":
                    h = gidx
                    for tt in (range(NT) if h > 0 else ()):
                        b = nextbank()
                        proj_fm(slot, 128, tt, b)
                        S.op("act", lambda e, b=b, tt=tt: e.activation(out=KT[:, tt * 512:(tt + 1) * 512], in_=P[b][:, :], func=AF.Copy),
                             r=[("P", b)], w=[("KT", tt)])
                    def v_tr(tt):
                        b2 = nextbank()
                        Pb = P[b2][:].bitcast(BF16)
                        vts = VTS[tt % 2]
                        for j in range(4):
                            S.op("pe", lambda e, j=j: e.transpose(out=Pb[:, j * 128:(j + 1) * 128], in_=vts[:, j * 128:(j + 1) * 128],
                                                                  identity=idb[:]),
                                 r=[("C", "VTS", tt % 2), "idb", cph], w=[("P", b2)])
                        S.op("dve", lambda e: e.tensor_copy(
                            out=Vt[:, tt * 4:(tt + 1) * 4, :], in_=Pb[:, 0:512].rearrange("p (j d) -> p j d", j=4)),
                            r=[("P", b2)], w=[("V", tt)])

                    for tt in range(NT):
                        b = nextbank()
                        proj_fm(slot, 256, tt, b)
                        S.op("act", lambda e, b=b, tt=tt: e.activation(out=VTS[tt % 2], in_=P[b][:, :], func=AF.Copy),
                             r=[("P", b), cph], w=[("C", "VTS", tt % 2)])
                        if tt > 0:
                            v_tr(tt - 1)
                    v_tr(NT - 1)

                    def qg(qt, h=h, slot=slot):
                        i2 = qt % 2
                        b = nextbank()
                        proj_fm(slot, 0, qt, b)
                        S.op("act", lambda e, b=b: e.activation(out=QT[i2], in_=P[b][:, :], func=AF.Copy),
                             r=[("P", b), cph], w=[("C", "QT", i2)])
                        b2 = nextbank()
                        proj_fm(slot, 384, qt, b2)
                        S.op("act", lambda e, b2=b2: e.activation(out=TH, in_=P[b2][:, :], func=AF.Tanh, scale=0.5),
                             r=[("P", b2), cph], w=[("C", "VTS", 0), ("C", "VTS", 1)])
                        S.op("dve", lambda e, b2=b2: e.scalar_tensor_tensor(out=SG[i2], in0=TH, scalar=1.0, in1=P[b2][:, :],
                                                                            op0=ALU.add, op1=ALU.mult),
                             r=[("P", b2), ("C", "VTS", 0), ("C", "VTS", 1), cph], w=[("C", "SG", i2)])
                        S.op("sp", lambda e: e.dma_start(out=CQ[i2], in_=cumd[h:h + 1, qt * 512:(qt + 1) * 512].partition_broadcast(128)),
                             r=["cumd", cph], w=[("C", "CQ", i2)], dma=True)

                    qg(0)
                    for qt in range(NT):
                        i2 = qt % 2
                        nb = 4 * qt + 4

                        def front(i, qt=qt, i2=i2, h=h):
                            kt = i
                            c0 = max(0, kt - 4 * qt) * 128
                            sbk = i % 3
                            diag = kt >= 4 * qt
                            S.op("pe", lambda e: e.matmul(P[sbk][:, c0:512], lhsT=KT[:, kt * 128:(kt + 1) * 128], rhs=QT[i2][:, c0:512],
                                                          start=True, stop=not diag),
                                 r=[("KT", kt // 4), ("C", "QT", i2), cph], w=[("P", sbk)])
                            if diag:
                                S.op("pe", lambda e: e.matmul(P[sbk][:, c0:c0 + 128], lhsT=idb[:], rhs=maskb[:], start=False, stop=True),
                                     r=["idb", "maskb"], w=[("P", sbk)])
                            ckc = (kt % 2) * 128 + h * 16 + kt // 2
                            S.op("dve", lambda e: e.scalar_tensor_tensor(out=TMP[i % 3][:, c0:512], in0=P[sbk][:, c0:512],
                                                                         scalar=csp2[:, ckc:ckc + 1],
                                                                         in1=CQ[i2][:, c0:512], op0=ALU.add, op1=ALU.subtract),
                                 r=[("P", sbk), ("C", "CQ", i2), "csp2", cph], w=[("C", "TMP", i % 3)])
                            S.op("act", lambda e: e.activation(out=PT[i % 4][:, c0:512], in_=TMP[i % 3][:, c0:512], func=AF.Exp, scale=SCALE),
                                 r=[("C", "TMP", i % 3), cph], w=[("C", "PT", i % 4)])

                        def back(i, qt=qt, nb=nb):
                            kt = i
                            c0 = max(0, kt - 4 * qt) * 128
                            S.op("pe", lambda e: e.matmul(P[3][:, c0:512], lhsT=Vt[:, kt, :], rhs=PT[i % 4][:, c0:512],
                                                          start=(i == 0), stop=(i == nb - 1)),
                                 r=[("V", kt // 4), ("C", "PT", i % 4), cph], w=[("P", 3)])
                            S.op("pe", lambda e: e.matmul(P[4][:, c0:512], lhsT=onesb[:], rhs=PT[i % 4][:, c0:512],
                                                          start=(i == 0), stop=(i == nb - 1)),
                                 r=["onesb", ("C", "PT", i % 4), cph], w=[("P", 4)])

                        for i_ in range(min(LA, nb)):
                            front(i_)
                        if qt + 1 < NT:
                            qg(qt + 1)
                        for i in range(nb):
                            if i + LA < nb:
                                front(i + LA)
                            back(i)
                        yi = yo_cnt[0] % 2
                        yo_cnt[0] += 1
                        S.op("dve", lambda e: e.reciprocal(out=RL, in_=P[4][:, :]), r=[("P", 4), cph], w=[("C", "RL")])
                        S.op("dve", lambda e: e.scalar_tensor_tensor(out=OS, in0=P[3][:, :], scalar=0.5, in1=RL, op0=ALU.mult, op1=ALU.mult),
                             r=[("P", 3), ("C", "RL"), cph], w=[("C", "OS")])
                        S.op("pool", lambda e, yi=yi, i2=i2: e.tensor_tensor(out=YO[yi], in0=OS, in1=SG[i2], op=ALU.mult),
                             r=[("C", "OS"), ("C", "SG", i2), cph], w=[("C", "YO", yi)])
                        S.op("sp", lambda e, yi=yi, h=h, qt=qt: e.dma_start(out=yTd[h * 128:(h + 1) * 128, qt * 512:(qt + 1) * 512], in_=YO[yi]),
                             r=[("C", "YO", yi), cph], w=[("yTd", h, qt)], dma=True)
                    if h == 7:
                        cbarrier()

                elif kind == "conv":
                    j = gidx
                    if j == 0:
                        for k_ in range(3):
                            S.op("sp", lambda e, l=l, k_=k_: _cw_load(nc, e, cwt[:, :, k_], conv_w[l, k_]), w=[("cwt", k_)], dma=True)
                    HS = Cf(0, 512)
                    SGc = Cf(512, 512)
                    U = [Cf(1024, 514), Cf(1538, 514)]
                    Y = Cf(2052, 512)
                    YOc = [Cb(2564, 512), Cb(2820, 512)]
                    for tt in range(NT):
                        bs = [0, 1, 2, 3] if tt % 2 == 0 else [4, 5, 6, 7]
                        for q in range(4):
                            proj_fm(slot, q * 128, tt, bs[q])
                        u = U[tt % 2]
                        un = U[(tt + 1) % 2]
                        uk = ("C", "U", tt % 2)
                        unk = ("C", "U", (tt + 1) % 2)
                        if tt == 0:
                            S.op("pool", lambda e, u=u: e.memset(u[:, 0:2], 0.0), r=[cph], w=[("C", "Uh", 0)])
                        S.op("act", lambda e, b=bs[2]: e.activation(out=HS, in_=P[b][:, :], func=AF.Copy), r=[("P", bs[2]), cph], w=[("C", "HS")])
                        S.op("act", lambda e, b=bs[3]: e.activation(out=SGc, in_=P[b][:, :], func=AF.Silu), r=[("P", bs[3]), cph], w=[("C", "SGc")])
                        S.op("dve", lambda e, b=bs[1], u=u: e.tensor_tensor(out=u[:, 2:514], in0=P[b][:, :], in1=HS, op=ALU.mult),
                             r=[("P", bs[1]), ("C", "HS"), cph], w=[uk])
                        if tt + 1 < NT:
                            S.op("pool", lambda e, u=u, un=un: e.tensor_copy(out=un[:, 0:2], in_=u[:, 512:514]),
                                 r=[uk, cph], w=[("C", "Uh", (tt + 1) % 2)])
                        uhk = ("C", "Uh", tt % 2)
                        S.op("dve", lambda e, u=u, j=j: e.tensor_scalar(out=Y, in0=u[:, 2:514], scalar1=cwt[:, j, 2:3], scalar2=None, op0=ALU.mult),
                             r=[uk, ("cwt", 0), ("cwt", 1), ("cwt", 2), cph], w=[("C", "Y")])
                        S.op("dve", lambda e, u=u, j=j: e.scalar_tensor_tensor(out=Y, in0=u[:, 1:513], scalar=cwt[:, j, 1:2], in1=Y,
                                                                               op0=ALU.mult, op1=ALU.add),
                             r=[uk, uhk, ("cwt", 0), ("cwt", 1), ("cwt", 2), ("C", "Y"), cph], w=[("C", "Y")])
                        S.op("dve", lambda e, u=u, j=j: e.scalar_tensor_tensor(out=Y, in0=u[:, 0:512], scalar=cwt[:, j, 0:1], in1=Y,
                                                                               op0=ALU.mult, op1=ALU.add),
                             r=[uk, uhk, ("cwt", 0), ("cwt", 1), ("cwt", 2), ("C", "Y"), cph], w=[("C", "Y")])
                        S.op("dve", lambda e, b=bs[0]: e.tensor_tensor(out=Y, in0=P[b][:, :], in1=Y, op=ALU.mult),
                             r=[("P", bs[0]), ("C", "Y"), cph], w=[("C", "Y")])
                        yi = tt % 2
                        S.op("dve", lambda e, yi=yi: e.tensor_tensor(out=YOc[yi], in0=Y, in1=SGc, op=ALU.mult),
                             r=[("C", "Y"), ("C", "SGc"), cph], w=[("C", "YOc", yi)])
                        S.op("sp", lambda e, yi=yi, j=j, tt=tt: e.dma_start(
                            out=yTd[1024 + j * 128:1024 + (j + 1) * 128, tt * 512:(tt + 1) * 512], in_=YOc[yi]),
                            r=[("C", "YOc", yi), cph], w=[("yTd", 8 + j, tt)], dma=True)
                    if j == 3:
                        cbarrier()

                else:
                    g0 = gidx
                    if g0 == 0:
                        for gg in range(4):
                            S.op("pool", lambda e, gg=gg, l=l: e.dma_start(out=pw[:, gg, :], in_=pool_w[l, gg]), w=[("pw", gg)], dma=True)
                        S.op("sp", lambda e, l=l: _ps_load(nc, e, pscale, pool_scale[l]), w=["pscale"], dma=True)
                    U0 = [Cf(0, 528), Cf(528, 528)]
                    SA = Cf(1056, 528)
                    SB = Cf(1584, 528)
                    SGp2 = [Cf(2112, 512), Cf(3712, 512)]
                    ZB2 = [Cb(2624, 512), Cb(3456, 512)]
                    YOp = [Cb(2880, 512), Cb(3136, 512)]
                    TC = Cf(3392, 16)
                    pend = []
                    for gg in range(2):
                        g_ = g0 + gg
                        wdw = WINDOWS[g_]
                        nst = g_ + 1
                        for tt in range(NT):
                            idx = gg * NT + tt
                            bs = [0, 1, 2] if idx % 2 == 0 else [3, 4, 5]
                            proj_fm(slot, gg * 256, tt, bs[0])
                            proj_fm(slot, gg * 256 + 128, tt, bs[1])
                            if pend:
                                pend.pop()()
                            ib = idx % 2
                            SGp = SGp2[ib]
                            ZB = ZB2[ib]
                            sgk = ("C", "SGp", ib)
                            zbk = ("C", "ZB", ib)
                            u = U0[tt % 2]
                            un = U0[(tt + 1) % 2]
                            uk = ("C", "U0", tt % 2)
                            uhk = ("C", "U0h", tt % 2)
                            if tt == 0:
                                S.op("pool", lambda e, u=u: e.memset(u[:, 0:16], 0.0), r=[cph], w=[uhk])
                            S.op("act", lambda e, b=bs[0], u=u: e.activation(out=u[:, 16:528], in_=P[b][:, :], func=AF.Copy),
                                 r=[("P", bs[0]), cph], w=[uk])
                            S.op("act", lambda e, b=bs[1], SGp=SGp: e.activation(out=SGp, in_=P[b][:, :], func=AF.Silu),
                                 r=[("P", bs[1]), cph], w=[sgk])
                            if tt + 1 < NT:
                                S.op("pool", lambda e, u=u, un=un: e.tensor_copy(out=un[:, 0:16], in_=u[:, 512:528]),
                                     r=[uk, cph], w=[("C", "U0h", (tt + 1) % 2)])
                            src = u
                            srck = [uk, uhk]
                            bufs = [(SA, ("C", "SA")), (SB, ("C", "SB"))]
                            sh = 1
                            lo = 1
                            for s_ in range(nst):
                                dst, dk_ = bufs[s_ % 2]
                                S.op("dve", lambda e, dst=dst, src=src, lo=lo, sh=sh: e.tensor_tensor(
                                    out=dst[:, lo:528], in0=src[:, lo:528], in1=src[:, lo - sh:528 - sh], op=ALU.add),
                                    r=srck + [cph], w=[dk_])
                                src, srck = dst, [dk_]
                                sh *= 2
                                lo += sh
                            aw = src
                            S.op("dve", lambda e, aw=aw, u=u, wdw=wdw, ZB=ZB: e.scalar_tensor_tensor(
                                out=ZB, in0=aw[:, 16:528], scalar=1.0 / wdw, in1=u[:, 16:528], op0=ALU.mult, op1=ALU.subtract),
                                r=srck + [uk, cph], w=[zbk])
                            if tt == 0:
                                n1 = wdw - 1
                                S.op("dve", lambda e, aw=aw, n1=n1: e.tensor_tensor(out=TC[:, 0:n1], in0=aw[:, 16:16 + n1], in1=invc[:, 0:n1], op=ALU.mult),
                                     r=srck + ["csts", cph], w=[("C", "TC")])
                                S.op("dve", lambda e, u=u, n1=n1, ZB=ZB: e.tensor_tensor(out=ZB[:, 0:n1], in0=TC[:, 0:n1], in1=u[:, 16:16 + n1], op=ALU.subtract),
                                     r=[("C", "TC"), uk, zbk, cph], w=[zbk])

                            def tail(g_=g_, b2=bs[2], yi=idx % 2, tt=tt, ZB=ZB, SGp=SGp, zbk=zbk, sgk=sgk):
                                S.op("pe", lambda e: e.matmul(P[b2][:, :], lhsT=pw[:, g_, :], rhs=ZB, start=True, stop=True),
                                     r=[("pw", g_), zbk, cph], w=[("P", b2)])
                                S.op("dve", lambda e: e.scalar_tensor_tensor(
                                    out=YOp[yi], in0=P[b2][:, :], scalar=pscale[:, g_:g_ + 1], in1=SGp, op0=ALU.mult, op1=ALU.mult),
                                    r=[("P", b2), "pscale", sgk, cph], w=[("C", "YOp", yi)])
                                S.op("sp", lambda e: e.dma_start(
                                    out=yTd[1536 + g_ * 128:1536 + (g_ + 1) * 128, tt * 512:(tt + 1) * 512], in_=YOp[yi]),
                                    r=[("C", "YOp", yi), cph], w=[("yTd", 12 + g_, tt)], dma=True)
                                if g_ == 3:
                                    if tt == 0:
                                        yload(0)
                                    if tt >= 4:
                                        woload(tt - 4)
                            pend.append(tail)
                    if pend:
                        pend.pop()()
                    if g0 == 2:
                        cbarrier()

            xload(0)
            S.op("sp", lambda e, l=l: e.dma_start(out=gbc, in_=ln_g[l:l + 1, :].partition_broadcast(128)), w=akeys(GBC, 4096), dma=True)
            S.op("sp", lambda e, l=l: e.dma_start(out=bbc, in_=ln_b[l:l + 1, :].partition_broadcast(128)), w=akeys(BBC, 4096), dma=True)


            def ln_tail(t, l=l, x_nxt=x_nxt):
                r_off = RR[t % 2]
                rr = Af32(r_off, 2048)
                rk = akeys(r_off, 4096)
                p2 = t % 2
                mvp = mv[:, p2, :]
                sttp = stt[:, p2, :]
                S.op("dve", lambda e: e.bn_aggr(out=mvp[:, 0:2], in_=sttp), r=[("stt", p2, c) for c in range(4)], w=[("mv01", p2)])
                S.op("act", lambda e: e.activation(out=mvp[:, 2:3], in_=mvp[:, 1:2], func=AF.Sqrt, bias=LN_EPS), r=[("mv01", p2)], w=[("mv2", p2)])
                S.op("dve", lambda e: e.reciprocal(out=mvp[:, 3:4], in_=mvp[:, 2:3]), r=[("mv2", p2)], w=[("mv3", p2)])
                S.op("dve", lambda e: e.tensor_scalar(out=mvp[:, 4:5], in0=mvp[:, 0:1], scalar1=mvp[:, 3:4], scalar2=-1.0,
                                                      op0=ALU.mult, op1=ALU.mult), r=[("mv01", p2), ("mv3", p2)], w=[("mv4", p2)])
                S.op("act", lambda e: e.activation(out=rr, in_=rr, func=AF.Identity, bias=mvp[:, 4:5], scale=mvp[:, 3:4]),
                     r=rk + [("mv3", p2), ("mv4", p2)], w=rk)
                S.op("dve", lambda e: e.tensor_tensor(out=rr, in0=rr, in1=gbc, op=ALU.mult), r=rk + akeys(GBC, 4096), w=rk)
                S.op("pool", lambda e: e.tensor_tensor(out=rr, in0=rr, in1=bbc, op=ALU.add), r=rk + akeys(BBC, 4096), w=rk)
                so = S.op("sp", lambda e: e.dma_start(out=x_nxt[t * 128:(t + 1) * 128, :], in_=rr),
                          r=rk, w=[("xres", l + 1, t)], dma=True)
                if l == L - 1:
                    out_ops.append(so)
                else:
                    emit_xT(t, rr, rk, t % 2)

            for tg in range(NT):
                if tg + 1 < NT:
                    yload(tg + 1)
                if tg == NT - 1 and l + 1 < L:
                    start_loads(l + 1)
                ys = tg % 2
                for ts in range(4):
                    t = tg * 4 + ts
                    if t + 1 < 32:
                        xload(t + 1)
                    xs_off = XS[t % 2]
                    xs = Af32(xs_off, 2048)
                    r_off = RR[t % 2]
                    rr = Af32(r_off, 2048)
                    for cg in range(4):
                        for kc in range(KC):
                            S.op("pe", lambda e, kc=kc, cg=cg, ys=ys, ts=ts: e.matmul(
                                P[cg][:, :], lhsT=Wt[:, ys, kc, ts * 128:(ts + 1) * 128], rhs=wo[:, kc, cg * 512:(cg + 1) * 512],
                                start=(kc == 0), stop=(kc == KC - 1)),
                                r=[("W", ys, q_) for q_ in range(4)] + akeys(WO + kc * 2048 + cg * 512, 512), w=[("P", cg)])
                    if t > 0:
                        ln_tail(t - 1)
                    for cg in range(4):
                        S.op("dve", lambda e, cg=cg, xs=xs, rr=rr: e.scalar_tensor_tensor(
                            out=rr[:, cg * 512:(cg + 1) * 512], in0=xs[:, cg * 512:(cg + 1) * 512], scalar=ALPHA, in1=P[cg][:, :],
                            op0=ALU.mult, op1=ALU.add),
                            r=[("P", cg)] + akeys(xs_off + cg * 1024, 1024), w=akeys(r_off + cg * 1024, 1024))
                        S.op("dve", lambda e, cg=cg, rr=rr, t=t: e.bn_stats(out=stt[:, t % 2, cg * 6:(cg + 1) * 6], in_=rr[:, cg * 512:(cg + 1) * 512]),
                             r=akeys(r_off + cg * 1024, 1024), w=[("stt", t % 2, cg)])
            if l < L - 1:
                for tt in range(2):
                    xt_reload(tt, 1)
            ln_tail(31)

        if debug:
            out_ops = [o for o in S.ops if o.dma]
        S.op("sp", None, extra=out_ops)
        S.run()
    return nc


def _cw_load(nc, e, cwt, src):
    with nc.allow_non_contiguous_dma(reason="tiny per-channel taps"):
        return e.dma_start(out=cwt, in_=src.rearrange("(j p) -> p j", p=128))


def _ps_load(nc, e, pscale, src):
    with nc.allow_non_contiguous_dma(reason="tiny per-channel scale"):
        return e.dma_start(out=pscale[:, :], in_=src.rearrange("(g p) -> p g", p=128))


def _consts():
    c = np.zeros((128, 272), np.float32)
    c[:, 0:128] = np.eye(128, dtype=np.float32)
    c[:, 128:256] = np.triu(np.ones((128, 128), np.float32))
    c[:, 256:272] = 1.0 / np.arange(1, 17, dtype=np.float32)[None, :]
    return c


_NC_CACHE = {}


def _get_nc(L):
    if L not in _NC_CACHE:
        _NC_CACHE[L] = build_nc(L)
    return _NC_CACHE[L]


def kernel(x, w_in, b_f, conv_w, pool_w, pool_scale, w_out, ln_g, ln_b):
    f = lambda a: np.ascontiguousarray(np.asarray(a, dtype=np.float32))
    x, w_in, b_f, conv_w, pool_w, pool_scale, w_out, ln_g, ln_b = map(
        f, (x, w_in, b_f, conv_w, pool_w, pool_scale, w_out, ln_g, ln_b))
    cst = _consts()
    n = 8
    if FUSED:
        nc = _get_nc(DEPTH)
        in_maps = [{"x": x[c], "w_in": w_in, "b_f": b_f, "conv_w": conv_w, "pool_w": pool_w, "pool_scale": pool_scale,
                    "w_out": w_out, "ln_g": ln_g, "ln_b": ln_b, "cst": cst} for c in range(n)]
        res = run_bass_kernel_spmd(nc, in_maps, core_ids=list(range(n)))
        return np.stack([res.results[c]["out"] for c in range(n)], axis=0)
    nc = _get_nc(1)
    cur = [x[c] for c in range(n)]
    for l in range(DEPTH):
        in_maps = [{"x": cur[c], "w_in": w_in[l:l + 1], "b_f": b_f[l:l + 1], "conv_w": conv_w[l:l + 1], "pool_w": pool_w[l:l + 1],
                    "pool_scale": pool_scale[l:l + 1], "w_out": w_out[l:l + 1], "ln_g": ln_g[l:l + 1], "ln_b": ln_b[l:l + 1],
                    "cst": cst} for c in range(n)]
        res = run_bass_kernel_spmd(nc, in_maps, core_ids=list(range(n)))
        cur = [np.asarray(res.results[c]["out"]) for c in range(n)]
    return np.stack(cur, axis=0)
```
